# Optimizing a Trainium2 kernel written in Bass

```python
import jax, jax.numpy as jnp
from jax import lax
import numpy as np

D_MODEL = 1024
BATCH = 4
SEQ = 8192
DEPTH = 1

N_Q_HEADS = 16
N_KV_HEADS = 2
HEAD_DIM = 64
Q_PER_KV = N_Q_HEADS // N_KV_HEADS
WINDOW = 128
BLOCK = 128
ATTN_WIDTH = N_Q_HEADS * HEAD_DIM
KV_WIDTH = N_KV_HEADS * HEAD_DIM
POOL_WINDOWS = (2, 4, 8, 16)
N_POOL_GROUPS = len(POOL_WINDOWS)
POOL_WIDTH = 512
POOL_GROUP = POOL_WIDTH // N_POOL_GROUPS
D_FF = 2816
NORM_EPS = 1e-6
IN_SPLITS = tuple(int(s) for s in np.cumsum([ATTN_WIDTH, KV_WIDTH, KV_WIDTH, POOL_WIDTH, D_MODEL]))
IN_WIDTH = ATTN_WIDTH + 2 * KV_WIDTH + POOL_WIDTH + 2 * D_MODEL

kernel_name = "hybrid_swa_sink_alibi_pool_macaron"


def alibi_slopes():
    h = np.arange(1, N_Q_HEADS + 1, dtype=np.float32)
    return jnp.asarray(2.0 ** (-8.0 * h / N_Q_HEADS), dtype=jnp.float32).reshape(N_KV_HEADS, Q_PER_KV)


def rmsnorm(x, g):
    xf = x.astype(jnp.float32)
    y = xf * lax.rsqrt(jnp.mean(xf * xf, axis=-1, keepdims=True) + NORM_EPS)
    return (y * g.astype(jnp.float32)).astype(x.dtype)


def swiglu(x, w_up, w_down):
    a, b = jnp.split(x @ w_up, 2, axis=-1)
    return (jax.nn.silu(a) * b) @ w_down


def sliding_window_attention(q, k, v, sinks):
    B, S, _ = q.shape
    nb = S // BLOCK
    q = q.reshape(B, nb, BLOCK, N_KV_HEADS, Q_PER_KV, HEAD_DIM)
    k = k.reshape(B, S, N_KV_HEADS, HEAD_DIM)
    v = v.reshape(B, S, N_KV_HEADS, HEAD_DIM)
    pad = jnp.zeros((B, BLOCK, N_KV_HEADS, HEAD_DIM), k.dtype)

    def band(t):
        cur = t.reshape(B, nb, BLOCK, N_KV_HEADS, HEAD_DIM)
        prev = jnp.concatenate([pad, t[:, :S - BLOCK]], axis=1).reshape(B, nb, BLOCK, N_KV_HEADS, HEAD_DIM)
        return jnp.concatenate([prev, cur], axis=2)

    kb, vb = band(k), band(v)
    scale = HEAD_DIM ** -0.5
    scores = jnp.einsum('bnqhgd,bnkhd->bnhgqk', q, kb, preferred_element_type=jnp.float32) * scale
    qi = jnp.arange(BLOCK)[:, None] + BLOCK
    kj = jnp.arange(2 * BLOCK)[None, :]
    dist = (qi - kj)
    blk = jnp.arange(nb)[:, None, None]
    valid = (dist >= 0)[None] & (dist < WINDOW)[None] & (blk * BLOCK - BLOCK + kj[None] >= 0)
    slopes = alibi_slopes()[:, :, None, None]
    scores = scores - slopes * dist.astype(jnp.float32)
    scores = jnp.where(valid[None, :, None, None], scores, -jnp.inf)
    sink = sinks.astype(jnp.float32).reshape(N_KV_HEADS, Q_PER_KV)[:, :, None, None]
    m = jnp.maximum(jnp.max(scores, axis=-1, keepdims=True), sink)
    p = jnp.exp(scores - m)
    probs = p / (jnp.sum(p, axis=-1, keepdims=True) + jnp.exp(sink - m))
    out = jnp.einsum('bnhgqk,bnkhd->bnqhgd', probs.astype(vb.dtype), vb)
    return out.reshape(B, S, ATTN_WIDTH)


def multiscale_pool(z, w_mix, scale):
    B, S, _ = z.shape
    zf = z.astype(jnp.float32)
    c = jnp.concatenate([jnp.zeros((B, 1, POOL_WIDTH), jnp.float32), jnp.cumsum(zf, axis=1)], axis=1)
    t = jnp.arange(S)
    outs = []
    for gi, w in enumerate(POOL_WINDOWS):
        cg = c[:, :, gi * POOL_GROUP:(gi + 1) * POOL_GROUP]
        prev = jnp.concatenate([jnp.zeros((B, w - 1, POOL_GROUP), jnp.float32), cg[:, :S - w + 1]], axis=1)
        cnt = jnp.minimum(t + 1, w).astype(jnp.float32)[None, :, None]
        outs.append((cg[:, 1:] - prev) / cnt)
    pooled = (jnp.concatenate(outs, axis=-1) - zf).astype(z.dtype)
    pooled = pooled.reshape(B, S, N_POOL_GROUPS, POOL_GROUP)
    mixed = jnp.einsum('bsgc,gcd->bsgd', pooled, w_mix).reshape(B, S, POOL_WIDTH)
    return mixed * scale


def setup_inputs(seed: int = 0) -> dict:
    key = jax.random.key(seed)
    ks = jax.random.split(key, 20)
    f32 = jnp.float32

    def nrm(k, shape, fan_in):
        return jax.random.normal(k, shape, f32) * (fan_in ** -0.5)

    def gain(k, shape):
        return 1.0 + 0.02 * jax.random.normal(k, shape, f32)

    L = DEPTH
    return {
        "x": jax.random.normal(ks[0], (BATCH, SEQ, D_MODEL), f32),
        "ffn1_norm": gain(ks[1], (L, D_MODEL)),
        "ffn1_w_up": nrm(ks[2], (L, D_MODEL, 2 * D_FF), D_MODEL),
        "ffn1_w_down": nrm(ks[3], (L, D_FF, D_MODEL), D_FF),
        "mix_norm": gain(ks[4], (L, D_MODEL)),
        "w_in": nrm(ks[5], (L, D_MODEL, IN_WIDTH), D_MODEL),
        "sinks": jax.random.normal(ks[6], (L, N_Q_HEADS), f32),
        "w_attn_up": nrm(ks[7], (L, ATTN_WIDTH, D_MODEL), ATTN_WIDTH),
        "pool_w_mix": nrm(ks[8], (L, N_POOL_GROUPS, POOL_GROUP, POOL_GROUP), POOL_GROUP),
        "pool_scale": gain(ks[9], (L, POOL_WIDTH)),
        "w_pool_up": nrm(ks[10], (L, POOL_WIDTH, D_MODEL), POOL_WIDTH),
        "w_out": nrm(ks[11], (L, D_MODEL, D_MODEL), D_MODEL),
        "ffn2_norm": gain(ks[12], (L, D_MODEL)),
        "ffn2_w_up": nrm(ks[13], (L, D_MODEL, 2 * D_FF), D_MODEL),
        "ffn2_w_down": nrm(ks[14], (L, D_FF, D_MODEL), D_FF),
        "final_norm": gain(ks[15], (D_MODEL,)),
    }


def reference(x, ffn1_norm, ffn1_w_up, ffn1_w_down, mix_norm, w_in, sinks, w_attn_up,
              pool_w_mix, pool_scale, w_pool_up, w_out, ffn2_norm, ffn2_w_up, ffn2_w_down,
              final_norm):
    h = x
    for l in range(DEPTH):
        h = h + 0.5 * swiglu(rmsnorm(h, ffn1_norm[l]), ffn1_w_up[l], ffn1_w_down[l])
        u = rmsnorm(h, mix_norm[l])
        q, k, v, z, g_attn, g_pool = jnp.split(u @ w_in[l], IN_SPLITS, axis=-1)
        a = sliding_window_attention(q, k, v, sinks[l]) @ w_attn_up[l]
        p = multiscale_pool(z, pool_w_mix[l], pool_scale[l]) @ w_pool_up[l]
        merged = jax.nn.sigmoid(g_attn) * a + jax.nn.sigmoid(g_pool) * p
        h = h + merged @ w_out[l]
        h = h + 0.5 * swiglu(rmsnorm(h, ffn2_norm[l]), ffn2_w_up[l], ffn2_w_down[l])
    return rmsnorm(h, final_norm)
```

```python
import numpy as np
from contextlib import ExitStack
import concourse.bass as bass
import concourse.mybir as mybir
from concourse.bass_utils import run_bass_kernel_spmd

F32 = mybir.dt.float32
BF16 = mybir.dt.bfloat16
AF = mybir.ActivationFunctionType
ALU = mybir.AluOpType

D = 1024
DFF = 2816
NKC = 8
NT = 4
TT = 1024
NCOL = 1152
TOK_CORE = 4096
HALO = 128
NS = 6
SLOT_ELEMS = 4096
SEM_CH = 3000
POOL_W = (2, 4, 8, 16)
EPS = 1e-6
FF_GROUPS = [(0, 2), (2, 4), (6, 4), (10, 4), (14, 4), (18, 4)]


class Instr:
    __slots__ = ("eng", "fn", "waits", "sem", "val", "awaited", "is_dma", "dsem")

    def __init__(self, eng, fn, waits, is_dma=False, dsem=None):
        self.eng = eng; self.fn = fn; self.waits = waits
        self.sem = None; self.val = None
        self.awaited = False; self.is_dma = is_dma; self.dsem = dsem


class Res:
    __slots__ = ("w", "r", "excl")

    def __init__(self, excl=False):
        self.w = {}; self.r = {}; self.excl = excl


class Prog:
    def __init__(self):
        self.q = {"pe": [], "act": [], "dve": [], "pool": [], "sp": []}
        self.marks = []

    def mark(self, label):
        self.marks.append((label, {k: len(v) for k, v in self.q.items()}))

    def emit(self, eng, fn, reads=(), writes=(), dsem=None):
        is_dma = dsem is not None
        waits = []
        for r in reads:
            for e2, tok in r.w.items():
                if not (e2 == eng and eng == "pe"):
                    waits.append(tok)
            if r.excl:
                for e2, tok in r.r.items():
                    if e2 != eng:
                        waits.append(tok)
        for w in writes:
            for e2, tok in w.w.items():
                if e2 != eng or eng != "pe":
                    waits.append(tok)
            for e2, tok in w.r.items():
                if e2 != eng or eng != "pe":
                    waits.append(tok)
        ins = Instr(eng, fn, waits, is_dma, dsem)
        for t in waits:
            t.awaited = True
        for r in reads:
            r.r[eng] = ins
        for w in writes:
            w.w[eng] = ins; w.r = {}
        self.q[eng].append(ins)
        return ins


def build_program(debug_stage=None, n_tiles=NT):
    nc = bass.Bass("TRN2", target_bir_lowering=False)
    P = Prog()

    def dram_in(name, shape):
        return nc.dram_tensor(name, list(shape), F32, kind="ExternalInput").ap()

    x_d = dram_in("x", (HALO + TOK_CORE, D))
    wup_d = [dram_in("w_up1", (D, 2 * DFF)), dram_in("w_up2", (D, 2 * DFF))]
    wdn_d = [dram_in("w_dn1", (DFF, D)), dram_in("w_dn2", (DFF, D))]
    win_d = dram_in("w_in", (D, 3840))
    wau_d = dram_in("w_au", (D, D))
    wpu_d = dram_in("w_pu", (512, D))
    wo_d = dram_in("w_o", (D, D))
    wmix_d = dram_in("w_mix", (4, 128, 128))
    emask_d = dram_in("emask", (128, 8 * 2 * 256))
    emlo_d = dram_in("emlo", (128, 8 * 256))
    ident_d = dram_in("ident", (128, 128))
    gcols_d = dram_in("gcols", (128, 24))
    gfin_d = dram_in("gfin", (128, D))
    sinkc_d = dram_in("sinkc", (128, 8))
    pscale_d = dram_in("pscale", (128, 4))
    pinv_d = dram_in("pinv", (128, 64))
    oneh_d = dram_in("oneh", (128, 64))
    y_d = nc.dram_tensor("y", [TOK_CORE, D], F32, kind="ExternalOutput").ap()

    with ExitStack() as es:
        E = es.enter_context

        def sb(name, shape, dt):
            return E(nc.sbuf_tensor(name, list(shape), dt))

        h = sb("h", (128, 9, D), F32)
        xs = sb("xs", (128, 2, D), BF16)
        xnT = sb("xnT", (128, NKC, NCOL), BF16)
        R1 = sb("R1", (128, 8, NCOL), BF16)
        R2 = sb("R2", (128, 4, NCOL), F32)
        pooledT = sb("pooledT", (128, 4, TT), BF16)
        mixedT = sb("mixedT", (128, 4, TT), BF16)
        KT = sb("KT", (128, 2, NCOL), BF16)
        V = sb("V", (128, 9, 2, 64), BF16)
        qT = sb("qT", (128, 2, TT), BF16)
        emask = sb("emask_s", (128, 8, 2, 256), BF16)
        emlo = sb("emlo_s", (128, 8, 256), BF16)
        T32 = sb("T32", (128, 4, 512), F32)
        PT = sb("PT", (128, 6, 512), BF16)
        junk = sb("junk", (128, D), BF16)
        dent = sb("dent", (128, 2, 256), F32)
        ident = sb("ident_s", (128, 128), BF16)
        ones = sb("ones", (128, 64), BF16)
        oneh = sb("oneh_s", (128, 64), BF16)
        gcols = sb("gcols_s", (128, 3, 8), F32)
        gfin = sb("gfin_s", (128, D), F32)
        esink = sb("esink", (128, 8), F32)
        pscale = sb("pscale_s", (128, 4), F32)
        pinv = sb("pinv_s", (128, 4, 16), F32)
        wmix = sb("wmix_s", (128, 4, 128), BF16)
        zcarry = sb("zcarry", (128, 4, 16), F32)
        stat = sb("stat", (128, 9, 4), F32)
        ring = sb("ring", (128, NS, SLOT_ELEMS), BF16)
        ps = E(nc.psum_tensor("ps", [128, 8, 512], F32))

        mergedT = R2[:].bitcast(BF16).rearrange("p a (b c) -> p (a b) c", b=2)

        r_h = [Res() for _ in range(9)]
        r_xs = [Res() for _ in range(2)]
        r_xnT = [Res() for _ in range(9)]
        r_hid = [[[Res() for _ in range(9)] for _ in range(4)] for _ in range(2)]
        r_attnT = [[Res() for _ in range(9)] for _ in range(8)]
        r_zb = [Res() for _ in range(4)]
        r_merged = [[Res() for _ in range(9)] for _ in range(8)]
        r_pooled = [Res() for _ in range(4)]
        r_mixed = [[Res(), Res()] for _ in range(4)]
        r_KT = [Res() for _ in range(9)]
        r_V = [Res() for _ in range(9)]
        r_qT = [Res() for _ in range(2)]
        r_T32 = [Res() for _ in range(4)]
        r_junk = Res()
        r_PT = [Res() for _ in range(6)]
        r_dent = [Res() for _ in range(2)]
        r_const = Res()
        r_stat = [Res() for _ in range(9)]
        r_zcarry = Res()
        r_bank = [Res(excl=True) for _ in range(8)]
        r_ring = [Res() for _ in range(NS)]

        n_eng_sems = 8
        eng_sems = {e: [E(nc.semaphore(f"s_{e}{k}")) for k in range(n_eng_sems)] for e in ("pe", "act", "dve")}
        ring_sems = [E(nc.semaphore(f"s_ring{k}")) for k in range(NS)]
        h_sems = [E(nc.semaphore(f"s_h{k}")) for k in range(9)]
        const_sems = {"pool": E(nc.semaphore("s_const_pool")), "sp": E(nc.semaphore("s_const_sp"))}
        dsem_count = {}

        def subs_of(blocks):
            out = []
            blocks = list(blocks)
            if blocks[0] == 0:
                out.append((0, 128, [0])); blocks = blocks[1:]
            for k in range(0, len(blocks), 4):
                bl = blocks[k:k + 4]
                out.append((bl[0] * 128, len(bl) * 128, bl))
            return out

        ring_pos = [0]

        def ring_load(dmas):
            s = ring_pos[0] % NS
            ring_pos[0] += 1
            slot = ring[:, s, :]
            for dst_fn, src in dmas:
                dst = dst_fn(slot)
                P.emit("pool", (lambda e, dst=dst, src=src: e.dma_start(out=dst, in_=src)),
                       writes=[r_ring[s]], dsem=ring_sems[s])
            return slot, r_ring[s]

        def wview(slot, nk, ncols):
            return slot[:, 0:nk * ncols].rearrange("p (k n) -> p k n", k=nk)

        def load_cols(src2d, c0, ncols, nk=NKC):
            src = src2d.rearrange("(k p) n -> p k n", p=128)[:, :, c0:c0 + ncols]
            slot, res = ring_load([(lambda sl: wview(sl, nk, ncols), src)])
            return wview(slot, nk, ncols), res

        t32_pos = [0]

        def t32_next():
            k = t32_pos[0] % 4
            t32_pos[0] += 1
            return T32[:, k, :], r_T32[k]

        def mm(out, lhsT, rhs, start, stop, reads, bank):
            P.emit("pe", (lambda e: e.matmul(out, lhsT=lhsT, rhs=rhs, start=start, stop=stop)),
                   reads=reads, writes=[r_bank[bank]])

        def const_dma(eng, dst, src):
            ins = P.emit(eng, (lambda e: e.dma_start(out=dst, in_=src)), dsem=const_sems[eng])
            r_const.w[eng] = ins

        const_dma("pool", emask[:].rearrange("p a b c -> p (a b c)"), emask_d[:, :])
        const_dma("pool", emlo[:].rearrange("p a c -> p (a c)"), emlo_d[:, :])
        const_dma("pool", ident[:], ident_d[:, :])
        const_dma("pool", oneh[:], oneh_d[:, :])
        const_dma("pool", wmix[:], wmix_d.rearrange("g c d -> c g d"))
        const_dma("sp", gcols[:].rearrange("p a b -> p (a b)"), gcols_d[:, :])
        const_dma("sp", gfin[:], gfin_d[:, :])
        const_dma("sp", esink[:], sinkc_d[:, :])
        const_dma("sp", pscale[:], pscale_d[:, :])
        const_dma("sp", pinv[:].rearrange("p a b -> p (a b)"), pinv_d[:, :])
        ins_ones = P.emit("dve", lambda e: e.memset(ones[:], 1.0))
        r_const.w["dve"] = ins_ones
        P.emit("dve", lambda e: e.memset(stat[:, :, 3], EPS), writes=r_stat)
        P.emit("dve", lambda e: e.memset(KT[:].rearrange("p a b -> p (a b)"), 0.0), writes=r_KT)
        P.emit("act", lambda e: e.activation(out=esink[:], in_=esink[:], func=AF.Exp), reads=[r_const], writes=[r_const])

        tp_bank = [0]
        NLAG = 3

        class NormPipe:
            def __init__(self, kind, out_rows=None, after_store=None):
                self.kind = kind; self.out_rows = out_rows; self.after_store = after_store
                self.pending = []; self.staged = []

            def pre(self):
                if len(self.pending) >= NLAG:
                    b = self.pending.pop(0)
                    self.staged.append((b, self._apply_a(b)))

            def step(self, b):
                self.pre()
                self.post(b)

            def post(self, b):
                while self.staged:
                    self._apply_b(*self.staged.pop(0))
                P.emit("act", (lambda e, b=b: e.activation(out=junk[:], in_=h[:, b, :], func=AF.Square,
                                                           scale=1.0 / 32.0, accum_out=stat[:, b, 0:1])),
                       reads=[r_h[b]], writes=[r_junk, r_stat[b]])
                P.emit("act", (lambda e, b=b: e.activation(out=stat[:, b, 1:2], in_=stat[:, b, 0:1], func=AF.Sqrt,
                                                           bias=stat[:, b, 3:4], scale=1.0)),
                       reads=[r_stat[b]], writes=[r_stat[b]])
                self.pending.append(b)

            def flush(self):
                while self.pending or self.staged:
                    while self.staged:
                        self._apply_b(*self.staged.pop(0))
                    if self.pending:
                        b = self.pending.pop(0)
                        self.staged.append((b, self._apply_a(b)))

            def _apply_a(self, b):
                P.emit("dve", (lambda e, b=b: e.reciprocal(out=stat[:, b, 2:3], in_=stat[:, b, 1:2])),
                       reads=[r_stat[b]], writes=[r_stat[b]])
                if self.kind == 3:
                    return None
                xb = tp_bank[0] % 2
                tp_bank[0] += 1
                P.emit("act", (lambda e, b=b, xb=xb: e.activation(out=xs[:, xb, :], in_=h[:, b, :], func=AF.Copy,
                                                                   scale=stat[:, b, 2:3])),
                       reads=[r_stat[b], r_h[b]], writes=[r_xs[xb]])
                return xb

            def _apply_b(self, b, xb):
                kind = self.kind
                if kind == 3:
                    P.emit("dve", (lambda e, b=b: e.scalar_tensor_tensor(out=h[:, b, :], in0=h[:, b, :], scalar=stat[:, b, 2:3],
                                                                         in1=gfin[:], op0=ALU.mult, op1=ALU.mult)),
                           reads=[r_stat[b], r_const, r_h[b]], writes=[r_h[b]])
                    row = self.out_rows(b)
                    P.emit("sp", (lambda e, b=b, row=row: e.dma_start(out=y_d[row:row + 128, :], in_=h[:, b, :])),
                           reads=[r_h[b]], dsem=h_sems[b])
                    if self.after_store is not None:
                        self.after_store(b)
                    return
                bank = xb
                pview = ps[:, bank, :].bitcast(BF16)
                for c in range(NKC):
                    P.emit("pe", (lambda e, c=c, xb=xb, pview=pview: e.transpose(out=pview[:, c * 128:(c + 1) * 128],
                                                                                 in_=xs[:, xb, c * 128:(c + 1) * 128],
                                                                                 identity=ident[:])),
                           reads=[r_xs[xb], r_const], writes=[r_bank[bank]])
                P.emit("dve", (lambda e, b=b, pview=pview, kind=kind: e.tensor_tensor(
                    out=xnT[:, :, b * 128:(b + 1) * 128],
                    in0=pview.rearrange("p (c t) -> p c t", c=NKC),
                    in1=gcols[:, kind, :].unsqueeze(2).to_broadcast([128, NKC, 128]), op=ALU.mult)),
                       reads=[r_bank[bank], r_const], writes=[r_xnT[b]])

        def norm(kind, blocks):
            np_ = NormPipe(kind)
            for b in blocks:
                np_.step(b)
            np_.flush()

        def ffn(which, blocks, pipe=None):
            subs = subs_of(blocks)
            wup = wup_d[which]
            wdn = wdn_d[which]
            units = [0]
            dn_units = [0]

            def up(gi):
                j0, nj = FF_GROUPS[gi]
                wa, ra = load_cols(wup, j0 * 128, nj * 128)
                wb, rb = load_cols(wup, DFF + j0 * 128, nj * 128)
                buf = gi % 2
                for j in range(nj):
                    for (c0, n, bl) in subs:
                        u = units[0]; units[0] += 1
                        ba = u % 2; bb = 2 + u % 2
                        xr = [r_xnT[b] for b in bl]
                        for kc in range(NKC):
                            mm(ps[:, ba, 0:n], wa[:, kc, j * 128:(j + 1) * 128], xnT[:, kc, c0:c0 + n], kc == 0, kc == NKC - 1,
                               [ra] + xr, ba)
                        for kc in range(NKC):
                            mm(ps[:, bb, 0:n], wb[:, kc, j * 128:(j + 1) * 128], xnT[:, kc, c0:c0 + n], kc == 0, kc == NKC - 1,
                               [rb] + xr, bb)
                        tv, tr = t32_next()
                        P.emit("act", (lambda e, tv=tv, ba=ba, n=n: e.activation(out=tv[:, 0:n], in_=ps[:, ba, 0:n], func=AF.Silu)),
                               reads=[r_bank[ba]], writes=[tr])
                        P.emit("dve", (lambda e, tv=tv, bb=bb, n=n, buf=buf, j=j, c0=c0: e.tensor_tensor(
                            out=R1[:, buf * 4 + j, c0:c0 + n], in0=ps[:, bb, 0:n], in1=tv[:, 0:n], op=ALU.mult)),
                               reads=[r_bank[bb], tr], writes=[r_hid[buf][j][b] for b in bl])

            def down(gis):
                last = gis[-1] == len(FF_GROUPS) - 1
                pieces = []
                for gi in gis:
                    j0, nj = FF_GROUPS[gi]
                    src = wdn.rearrange("(k p) n -> p k n", p=128)[:, j0:j0 + nj, :]
                    slot, rd = ring_load([(lambda sl, nj=nj: wview(sl, nj, D), src)])
                    pieces.append((gi % 2, nj, wview(slot, nj, D), rd))
                ntot = sum(p[1] for p in pieces)
                for b in blocks:
                    if last and pipe is not None:
                        pipe.pre()
                    v = dn_units[0]; dn_units[0] += 1
                    b0 = 4 + 2 * (v % 2)
                    k = 0
                    for (buf, nj, wd, rd) in pieces:
                        for j in range(nj):
                            for hf in range(2):
                                mm(ps[:, b0 + hf, :], R1[:, buf * 4 + j, b * 128:(b + 1) * 128], wd[:, j, hf * 512:(hf + 1) * 512],
                                   k == 0, k == ntot - 1, [rd, r_hid[buf][j][b]], b0 + hf)
                            k += 1
                    for hf in range(2):
                        P.emit("dve", (lambda e, b=b, b0=b0, hf=hf: e.scalar_tensor_tensor(
                            out=h[:, b, hf * 512:(hf + 1) * 512], in0=ps[:, b0 + hf, :], scalar=0.5, in1=h[:, b, hf * 512:(hf + 1) * 512],
                            op0=ALU.mult, op1=ALU.add)),
                               reads=[r_bank[b0 + hf], r_h[b]], writes=[r_h[b]])
                    if last and pipe is not None:
                        pipe.post(b)

            ng = len(FF_GROUPS)
            up(0)
            for gi in range(1, ng):
                up(gi)
                if gi - 1 < ng - 2:
                    down([gi - 1])
            down([ng - 2, ng - 1])

        def mixer(ti, pipe=None):
            blocks_all = list(range(0, 9)) if ti == 0 else list(range(1, 9))
            main = list(range(1, 9))
            subs_all = subs_of(blocks_all)
            subs_main = subs_of(main)
            bank_rr = [0]

            def nb():
                k = bank_rr[0] % 4
                bank_rr[0] += 1
                return k

            P.mark(f"t{ti}.kvz")
            if ti > 0:
                P.emit("act", lambda e: e.activation(out=KT[:, :, 0:128], in_=KT[:, :, 1024:1152], func=AF.Copy),
                       reads=[r_KT[8]], writes=[r_KT[0]])
                P.emit("act", lambda e: e.activation(out=V[:, 0].rearrange("p a b -> p (a b)"),
                                                     in_=V[:, 8].rearrange("p a b -> p (a b)"), func=AF.Copy),
                       reads=[r_V[8]], writes=[r_V[0]])
            wsrc = win_d.rearrange("(k p) n -> p k n", p=128)
            dm = []
            for kvh in range(2):
                for dup in range(2):
                    dm.append((lambda sl, kvh=kvh, dup=dup: sl[:, 0:NKC * 256].rearrange("p (k n) -> p k n", k=NKC)
                               [:, :, kvh * 128 + dup * 64: kvh * 128 + dup * 64 + 64],
                               wsrc[:, :, 1024 + kvh * 64: 1024 + kvh * 64 + 64]))
            dm.append((lambda sl: sl[:, NKC * 256:NKC * 384].rearrange("p (k n) -> p k n", k=NKC), wsrc[:, :, 1152:1280]))
            slot, rkv = ring_load(dm)
            wk = slot[:, 0:NKC * 256].rearrange("p (k n) -> p k n", k=NKC)
            wv = slot[:, NKC * 256:NKC * 384].rearrange("p (k n) -> p k n", k=NKC)
            def kproj(kvh):
                for (c0, n, bl) in subs_all:
                    bk = nb()
                    for kc in range(NKC):
                        mm(ps[:, bk, 0:n], wk[:, kc, kvh * 128:(kvh + 1) * 128], xnT[:, kc, c0:c0 + n], kc == 0, kc == NKC - 1,
                           [rkv] + [r_xnT[b] for b in bl], bk)
                    P.emit("act", (lambda e, bk=bk, n=n, c0=c0, kvh=kvh: e.activation(out=KT[:, kvh, c0:c0 + n], in_=ps[:, bk, 0:n],
                                                                                     func=AF.Copy)),
                           reads=[r_bank[bk]], writes=[r_KT[b] for b in bl])
            def vproj():
                for b in blocks_all:
                    bk = nb()
                    for kc in range(NKC):
                        mm(ps[:, bk, 0:128], xnT[:, kc, b * 128:(b + 1) * 128], wv[:, kc, :], kc == 0, kc == NKC - 1, [rkv, r_xnT[b]], bk)
                    P.emit("act", (lambda e, bk=bk, b=b: e.activation(out=V[:, b].rearrange("p a d -> p (a d)"), in_=ps[:, bk, 0:128], func=AF.Copy)),
                           reads=[r_bank[bk]], writes=[r_V[b]])
            wz, rz = load_cols(win_d, 1280, 512)
            A = R2[:, 2, :]
            B = R2[:, 3, :]
            def zproj(g):
                zb = R2[:, g % 2, :]
                rzb = r_zb[g % 2]
                w = POOL_W[g]
                if ti > 0:
                    P.emit("dve", (lambda e, zb=zb, g=g: e.tensor_copy(out=zb[:, 112:128], in_=zcarry[:, g, :])),
                           reads=[r_zcarry], writes=[rzb])
                for (c0, n, bl) in subs_all:
                    bk = nb()
                    for kc in range(NKC):
                        mm(ps[:, bk, 0:n], wz[:, kc, g * 128:(g + 1) * 128], xnT[:, kc, c0:c0 + n], kc == 0, kc == NKC - 1,
                           [rz] + [r_xnT[b] for b in bl], bk)
                    P.emit("act", (lambda e, bk=bk, n=n, c0=c0, zb=zb: e.activation(out=zb[:, c0:c0 + n], in_=ps[:, bk, 0:n], func=AF.Copy)),
                           reads=[r_bank[bk]], writes=[rzb])
                src = zb; rsrc = rzb
                nst = g + 1
                for k in range(1, nst + 1):
                    sh = 1 << (k - 1)
                    lo = 112 + (1 << k) - 1
                    dst, rdst = (A, r_zb[2]) if k % 2 == 1 else (B, r_zb[3])
                    P.emit("dve", (lambda e, dst=dst, src=src, lo=lo, sh=sh: e.tensor_tensor(
                        out=dst[:, lo:NCOL], in0=src[:, lo:NCOL], in1=src[:, lo - sh:NCOL - sh], op=ALU.add)),
                           reads=[rsrc], writes=[rdst])
                    src = dst; rsrc = rdst
                pb = g
                P.emit("dve", (lambda e, src=src, zb=zb, pb=pb, w=w: e.scalar_tensor_tensor(
                    out=pooledT[:, pb, 16:TT], in0=src[:, 144:NCOL], scalar=1.0 / w, in1=zb[:, 144:NCOL],
                    op0=ALU.mult, op1=ALU.subtract)),
                       reads=[rsrc, rzb], writes=[r_pooled[pb]])
                if ti == 0:
                    tv, tr = t32_next()
                    P.emit("dve", (lambda e, tv=tv, src=src, g=g: e.tensor_tensor(out=tv[:, 0:16], in0=src[:, 128:144], in1=pinv[:, g, :],
                                                                                 op=ALU.mult)),
                           reads=[rsrc, r_const], writes=[tr])
                    P.emit("dve", (lambda e, tv=tv, zb=zb, pb=pb: e.tensor_tensor(out=pooledT[:, pb, 0:16], in0=tv[:, 0:16],
                                                                                  in1=zb[:, 128:144], op=ALU.subtract)),
                           reads=[tr, rzb], writes=[r_pooled[pb]])
                else:
                    P.emit("dve", (lambda e, src=src, zb=zb, pb=pb, w=w: e.scalar_tensor_tensor(
                        out=pooledT[:, pb, 0:16], in0=src[:, 128:144], scalar=1.0 / w, in1=zb[:, 128:144],
                        op0=ALU.mult, op1=ALU.subtract)),
                           reads=[rsrc, rzb], writes=[r_pooled[pb]])
                P.emit("dve", (lambda e, zb=zb, g=g: e.tensor_copy(out=zcarry[:, g, :], in_=zb[:, 1136:1152])),
                       reads=[rzb], writes=[r_zcarry])

            def mixmm(g):
                pb = g
                for si in range(2):
                    bk = nb()
                    mm(ps[:, bk, :], wmix[:, g, :], pooledT[:, pb, si * 512:(si + 1) * 512], True, True, [r_const, r_pooled[pb]], bk)
                    P.emit("act", (lambda e, bk=bk, g=g, si=si: e.activation(out=mixedT[:, g, si * 512:(si + 1) * 512], in_=ps[:, bk, :],
                                                                            func=AF.Copy, scale=pscale[:, g:g + 1])),
                           reads=[r_bank[bk], r_const], writes=[r_mixed[g][si]])


            zproj(0)
            zproj(1)
            kproj(0)
            zproj(2)
            kproj(1)
            zproj(3)
            vproj()
            for g in range(4):
                mixmm(g)

            wq = [None, None]
            rq = [None, None]
            set_ctr = [0]

            def qproj(c):
                if c % 4 == 0:
                    wq[0], rq[0] = load_cols(win_d, (c // 4) * 512, 512)
                qb = c % 2
                sset = set_ctr[0] % 3
                set_ctr[0] += 1
                for si in range(2):
                    bq = sset + 3 * si
                    for kc in range(NKC):
                        mm(ps[:, bq, :], wq[0][:, kc, (c % 4) * 128:(c % 4 + 1) * 128], xnT[:, kc, 128 + si * 512:128 + (si + 1) * 512],
                           kc == 0, kc == NKC - 1, [rq[0]] + [r_xnT[b] for b in range(1 + si * 4, 5 + si * 4)], bq)
                for si in range(2):
                    bq = sset + 3 * si
                    P.emit("act", (lambda e, qb=qb, si=si, bq=bq: e.activation(out=qT[:, qb, si * 512:(si + 1) * 512], in_=ps[:, bq, :], func=AF.Copy)),
                           reads=[r_bank[bq]], writes=[r_qT[qb]])

            att_u = [0]

            def qk(c, up):
                u = att_u[0]
                kvh = c // 4
                qb = c % 2
                sset = set_ctr[0] % 3
                set_ctr[0] += 1
                b = 1 + 2 * up
                for seg in range(4):
                    qblk = b + seg // 2
                    kblk = qblk - 1 + seg % 2
                    for half in range(2):
                        bank = sset + 3 * half
                        p0 = half * 64
                        mm(ps[:, bank, seg * 128:(seg + 1) * 128], KT[p0:p0 + 64, kvh, kblk * 128:(kblk + 1) * 128],
                           qT[p0:p0 + 64, qb, (qblk - 1) * 128:qblk * 128], True, False, [r_KT[kblk], r_qT[qb]], bank)
                    for half in range(2):
                        bank = sset + 3 * half
                        mm(ps[:, bank, seg * 128:(seg + 1) * 128], ident[:], emask[:, c, half, (seg % 2) * 128:(seg % 2 + 1) * 128],
                           False, half == 1, [r_const], bank)
                        if half == 0:
                            mm(ps[:, bank, seg * 128:(seg + 1) * 128], ident[:], emlo[:, c, (seg % 2) * 128:(seg % 2 + 1) * 128],
                               False, True, [r_const], bank)
                for half in range(2):
                    bank = sset + 3 * half
                    pk = (u % 3) * 2 + half
                    P.emit("act", (lambda e, pk=pk, bank=bank: e.activation(out=PT[:, pk, :], in_=ps[:, bank, :], func=AF.Exp, scale=0.125)),
                           reads=[r_bank[bank]], writes=[r_PT[pk]])
                att_u[0] += 1
                return (u, c, up)

            def pv(tok):
                u, c, up = tok
                kvh = c // 4
                par = u % 2
                b = 1 + 2 * up
                bank = 6 + par
                for what in range(2):
                    for qi in range(2):
                        qblk = b + qi
                        col = what * 256 + qi * 128
                        for kk in range(2):
                            kblk = qblk - 1 + kk
                            seg = qi * 2 + kk
                            for half in range(2):
                                pk = (u % 3) * 2 + half
                                if what == 0:
                                    lhsT = V[:, kblk, kvh, :]
                                    rd = [r_V[kblk], r_PT[pk]]
                                else:
                                    lhsT = oneh[:] if (ti == 0 and kblk == 0) else ones[:]
                                    rd = [r_const, r_PT[pk]]
                                mm(ps[half * 64:(half + 1) * 64, bank, col:col + 128], lhsT, PT[:, pk, seg * 128:(seg + 1) * 128],
                                   kk == 0, kk == 1, rd, bank)
                dk = par
                P.emit("dve", (lambda e, dk=dk, bank=bank, c=c: e.tensor_scalar_add(out=dent[:, dk, :], in0=ps[:, bank, 256:512],
                                                                                   scalar1=esink[:, c:c + 1])),
                       reads=[r_bank[bank], r_const], writes=[r_dent[dk]])
                P.emit("dve", (lambda e, dk=dk: e.reciprocal(out=dent[:, dk, :], in_=dent[:, dk, :])),
                       reads=[r_dent[dk]], writes=[r_dent[dk]])
                P.emit("dve", (lambda e, dk=dk, bank=bank, c=c, b=b: e.tensor_tensor(out=R1[:, c, b * 128:(b + 2) * 128], in0=ps[:, bank, 0:256],
                                                                                    in1=dent[:, dk, :], op=ALU.mult)),
                       reads=[r_bank[bank], r_dent[dk]], writes=[r_attnT[c][b], r_attnT[c][b + 1]])

            P.mark(f"t{ti}.attn")
            qproj(0)
            seq = [(c, up) for c in range(8) for up in range(4)]
            toks = []
            for n, (c, up) in enumerate(seq):
                if up == 2 and c < 7:
                    qproj(c + 1)
                toks.append(qk(c, up))
                if n >= 2:
                    pv(toks[n - 2])
            pv(toks[-2])
            pv(toks[-1])

            P.mark(f"t{ti}.m4")
            for mg in range(2):
                wau, rau = load_cols(wau_d, mg * 512, 512)
                wga, rga = load_cols(win_d, 1792 + mg * 512, 512)
                wpu, rpu = load_cols(wpu_d, mg * 512, 512, nk=4)
                wgp, rgp = load_cols(win_d, 2816 + mg * 512, 512)
                for mi in range(4):
                    m = mg * 4 + mi
                    ms = slice(mi * 128, (mi + 1) * 128)
                    for si, (c0, n, bl) in enumerate(subs_main):
                        par = (mi * 2 + si) % 2
                        ba, bga, bp, bgp = (par * 4 + k for k in range(4))
                        xr = [r_xnT[b] for b in bl]
                        for kc in range(NKC):
                            mm(ps[:, ba, :], wau[:, kc, ms], R1[:, kc, c0:c0 + n], kc == 0, kc == NKC - 1, [rau] + [r_attnT[kc][b] for b in bl], ba)
                        for kc in range(NKC):
                            mm(ps[:, bga, :], wga[:, kc, ms], xnT[:, kc, c0:c0 + n], kc == 0, kc == NKC - 1, [rga] + xr, bga)
                        for kc in range(4):
                            mm(ps[:, bp, :], wpu[:, kc, ms], mixedT[:, kc, si * 512:(si + 1) * 512], kc == 0, kc == 3, [rpu, r_mixed[kc][si]], bp)
                        for kc in range(NKC):
                            mm(ps[:, bgp, :], wgp[:, kc, ms], xnT[:, kc, c0:c0 + n], kc == 0, kc == NKC - 1, [rgp] + xr, bgp)
                        t1v, t1r = t32_next()
                        t2v, t2r = t32_next()
                        P.emit("act", (lambda e, t1v=t1v, bga=bga: e.activation(out=t1v, in_=ps[:, bga, :], func=AF.Sigmoid)),
                               reads=[r_bank[bga]], writes=[t1r])
                        P.emit("act", (lambda e, t2v=t2v, bgp=bgp: e.activation(out=t2v, in_=ps[:, bgp, :], func=AF.Sigmoid)),
                               reads=[r_bank[bgp]], writes=[t2r])
                        P.emit("dve", (lambda e, t1v=t1v, ba=ba: e.tensor_tensor(out=t1v, in0=ps[:, ba, :], in1=t1v, op=ALU.mult)),
                               reads=[r_bank[ba], t1r], writes=[t1r])
                        P.emit("dve", (lambda e, t2v=t2v, bp=bp: e.tensor_tensor(out=t2v, in0=ps[:, bp, :], in1=t2v, op=ALU.mult)),
                               reads=[r_bank[bp], t2r], writes=[t2r])
                        P.emit("dve", (lambda e, t1v=t1v, t2v=t2v, m=m, c0=c0, n=n: e.tensor_tensor(out=mergedT[:, m, c0:c0 + n], in0=t1v, in1=t2v,
                                                                                                  op=ALU.add)),
                               reads=[t1r, t2r], writes=[r_merged[m][b] for b in bl])
            P.mark(f"t{ti}.m5")
            wo0, ro0 = load_cols(wo_d, 0, 512)
            wo1, ro1 = load_cols(wo_d, 512, 512)
            for bi, b in enumerate(main):
                if pipe is not None:
                    pipe.pre()
                b0 = 4 + 2 * (bi % 2)
                for hf, (wo, ro) in enumerate(((wo0, ro0), (wo1, ro1))):
                    for kc in range(NKC):
                        mm(ps[:, b0 + hf, :], mergedT[:, kc, b * 128:(b + 1) * 128], wo[:, kc, :], kc == 0, kc == NKC - 1,
                           [ro, r_merged[kc][b]], b0 + hf)
                for hf in range(2):
                    P.emit("dve", (lambda e, b=b, b0=b0, hf=hf: e.tensor_tensor(out=h[:, b, hf * 512:(hf + 1) * 512], in0=ps[:, b0 + hf, :],
                                                                               in1=h[:, b, hf * 512:(hf + 1) * 512], op=ALU.add)),
                           reads=[r_bank[b0 + hf], r_h[b]], writes=[r_h[b]])
                if pipe is not None:
                    pipe.post(b)

        def dump_h(ti):
            for b in range(1, 9):
                row = ti * TT + (b - 1) * 128
                P.emit("sp", (lambda e, b=b, row=row: e.dma_start(out=y_d[row:row + 128, :], in_=h[:, b, :])),
                       reads=[r_h[b]], dsem=h_sems[b])

        def load_x(ti, b):
            row = ti * TT + b * 128
            P.emit("sp", (lambda e, b=b, row=row: e.dma_start(out=h[:, b, :], in_=x_d[row:row + 128, :])),
                   writes=[r_h[b]], dsem=h_sems[b])

        n0 = None
        xq = []
        XLAG = 3
        for ti in range(n_tiles):
            blocks_all = list(range(0, 9)) if ti == 0 else list(range(1, 9))
            main = list(range(1, 9))
            P.mark(f"t{ti}.norm0")
            if n0 is None:
                for b in blocks_all:
                    load_x(ti, b)
                norm(0, blocks_all)
            else:
                while xq:
                    n0.step(xq.pop(0))
                n0.flush()
            P.mark(f"t{ti}.ffn1")
            if debug_stage == 1:
                ffn(0, blocks_all)
                dump_h(ti); n0 = None; continue
            n1 = NormPipe(1)
            ffn(0, blocks_all, pipe=n1)
            P.mark(f"t{ti}.norm1")
            n1.flush()
            n2 = NormPipe(2)
            mixer(ti, pipe=(None if debug_stage == 2 else n2))
            if debug_stage == 2:
                dump_h(ti); n0 = None; continue
            P.mark(f"t{ti}.norm2")
            n2.flush()
            P.mark(f"t{ti}.ffn2")
            if ti + 1 < n_tiles:
                n0 = NormPipe(0)

                def after_store(b, ti=ti, n0=n0):
                    load_x(ti + 1, b)
                    xq.append(b)
                    if len(xq) > XLAG:
                        n0.step(xq.pop(0))
            else:
                n0 = None
                after_store = None
            n3 = NormPipe(3, out_rows=(lambda b, ti=ti: ti * TT + (b - 1) * 128), after_store=after_store)
            ffn(1, main, pipe=n3)
            n3.flush()

        for eng in ("pe", "act", "dve"):
            n = 0
            for ins in P.q[eng]:
                if ins.awaited:
                    ins.sem = eng_sems[eng][n // SEM_CH]
                    ins.val = n % SEM_CH + 1
                    n += 1
            assert n <= SEM_CH * n_eng_sems, (eng, n)
        for eng in ("pool", "sp"):
            n_const = sum(1 for ins in P.q[eng] if ins.dsem is const_sems[eng])
            for ins in P.q[eng]:
                k = id(ins.dsem)
                dsem_count[k] = dsem_count.get(k, 0) + 16
                ins.sem = ins.dsem
                ins.val = 16 * n_const if ins.dsem is const_sems[eng] else dsem_count[k]
        last_out = {}
        for ins in P.q["sp"]:
            last_out[id(ins.sem)] = ins

        block = E(nc.Block())

        def replay(engname):
            def run(e):
                waited = {}
                for ins in P.q[engname]:
                    for t in ins.waits:
                        k = id(t.sem)
                        if waited.get(k, 0) < t.val:
                            e.wait_ge(t.sem, t.val)
                            waited[k] = t.val
                    bi = ins.fn(e)
                    if ins.is_dma:
                        bi.then_inc(ins.sem, 16)
                    elif ins.awaited:
                        bi.then_inc(ins.sem, 1)
                if engname == "sp":
                    for ins in last_out.values():
                        e.wait_ge(ins.sem, ins.val)
            return run

        block.gpsimd(replay("pool"))
        block.sync(replay("sp"))
        block.scalar(replay("act"))
        block.vector(replay("dve"))
        block.tensor(replay("pe"))
    nc._marks = P.marks
    return nc


def _alibi_slopes():
    hh = np.arange(1, 17, dtype=np.float32)
    return (2.0 ** (-8.0 * hh / 16)).astype(np.float32)


def _bf16_round(a):
    u = np.ascontiguousarray(a, dtype=np.float32).view(np.uint32).astype(np.uint64)
    u = (u + 0x7FFF + ((u >> 16) & 1)) & 0xFFFF0000
    return u.astype(np.uint32).view(np.float32)


def _host_constants():
    sl = _alibi_slopes()
    k = np.arange(128)[:, None].astype(np.float32)
    q = np.arange(128)[None, :].astype(np.float32)
    em = np.zeros((128, 8, 2, 256), np.float32)
    for c in range(8):
        for half in range(2):
            s = sl[2 * c + half]
            dprev = 128.0 + q - k
            dcur = q - k
            em[:, c, half, 0:128] = np.where(dprev < 128, -8.0 * s * dprev, -240000.0)
            em[:, c, half, 128:256] = np.where(dcur >= 0, -8.0 * s * dcur, -240000.0)
    hi = _bf16_round(em)
    lo = np.where(em <= -200000.0, 0.0, _bf16_round(em - hi)).astype(np.float32)
    assert np.all(lo[:, :, 1, :] == 0.0)
    return hi.reshape(128, -1), np.ascontiguousarray(lo[:, :, 0, :]).reshape(128, -1), np.eye(128, dtype=np.float32)


_NC_CACHE = {}


def kernel(x, ffn1_norm, ffn1_w_up, ffn1_w_down, mix_norm, w_in, sinks, w_attn_up,
           pool_w_mix, pool_scale, w_pool_up, w_out, ffn2_norm, ffn2_w_up, ffn2_w_down,
           final_norm, _debug_stage=None, _n_tiles=NT):
    f = lambda a: np.ascontiguousarray(np.asarray(a, dtype=np.float32))
    x = f(x)
    Bn, S, _ = x.shape
    em, emlo, ident = _host_constants()
    gcols = np.stack([f(ffn1_norm)[0].reshape(8, 128).T, f(mix_norm)[0].reshape(8, 128).T,
                      f(ffn2_norm)[0].reshape(8, 128).T], axis=1).reshape(128, 24)
    gfin = np.ascontiguousarray(np.broadcast_to(f(final_norm)[None, :], (128, D)))
    sk = f(sinks)[0]
    sinkc = np.ascontiguousarray(np.repeat(sk.reshape(8, 2), 64, axis=1).T)
    pscale = np.ascontiguousarray(f(pool_scale)[0].reshape(4, 128).T)
    shared = {
        "w_up1": f(ffn1_w_up)[0], "w_dn1": f(ffn1_w_down)[0], "w_up2": f(ffn2_w_up)[0], "w_dn2": f(ffn2_w_down)[0],
        "w_in": f(w_in)[0], "w_au": f(w_attn_up)[0], "w_pu": f(w_pool_up)[0], "w_o": f(w_out)[0],
        "w_mix": f(pool_w_mix)[0], "emask": em, "emlo": emlo, "ident": ident, "gcols": np.ascontiguousarray(gcols), "gfin": gfin,
        "sinkc": sinkc, "pscale": pscale,
    }
    in_maps = []
    for core in range(8):
        b, half = core // 2, core % 2
        xc = np.zeros((HALO + TOK_CORE, D), np.float32)
        if half == 0:
            xc[HALO:] = x[b, 0:TOK_CORE]
        else:
            xc[:] = x[b, TOK_CORE - HALO:S]
        pinv = np.zeros((128, 4, 16), np.float32)
        for g, w in enumerate(POOL_W):
            t = np.arange(16)
            pinv[:, g, :] = (1.0 / np.minimum(t + 1, w)) if half == 0 else (1.0 / w)
        oneh = np.full((128, 64), 1.0 if half == 1 else 0.0, np.float32)
        m = dict(shared)
        m["x"] = xc
        m["pinv"] = pinv.reshape(128, 64)
        m["oneh"] = oneh
        in_maps.append(m)
    key = (_debug_stage, _n_tiles)
    if key not in _NC_CACHE:
        _NC_CACHE[key] = build_program(_debug_stage, _n_tiles)
    nc = _NC_CACHE[key]
    res = run_bass_kernel_spmd(nc, in_maps, core_ids=list(range(8)))
    out = np.empty((Bn, S, D), np.float32)
    for core in range(8):
        b, half = core // 2, core % 2
        out[b, half * TOK_CORE:(half + 1) * TOK_CORE] = res.results[core]["y"]
    return out
```

```python
import numpy as np
from contextlib import ExitStack
import concourse.bass as bass
import concourse.mybir as mybir
from concourse.bass_utils import run_bass_kernel_spmd

F32 = mybir.dt.float32
BF16 = mybir.dt.bfloat16
AF = mybir.ActivationFunctionType
ALU = mybir.AluOpType

D = 1024
DFF = 2816
NKC = 8
NT = 4
TT = 1024
NCOL = 1152
TOK_CORE = 4096
HALO = 128
NS = 6
SLOT_ELEMS = 4096
SEM_CH = 3000
POOL_W = (2, 4, 8, 16)
EPS = 1e-6
FF_GROUPS = [(0, 2), (2, 4), (6, 4), (10, 4), (14, 4), (18, 4)]


class Instr:
    __slots__ = ("eng", "fn", "waits", "sem", "val", "awaited", "is_dma", "dsem")

    def __init__(self, eng, fn, waits, is_dma=False, dsem=None):
        self.eng = eng; self.fn = fn; self.waits = waits
        self.sem = None; self.val = None
        self.awaited = False; self.is_dma = is_dma; self.dsem = dsem


class Res:
    __slots__ = ("w", "r", "excl")

    def __init__(self, excl=False):
        self.w = {}; self.r = {}; self.excl = excl


class Prog:
    def __init__(self):
        self.q = {"pe": [], "act": [], "dve": [], "pool": [], "sp": []}
        self.marks = []

    def mark(self, label):
        self.marks.append((label, {k: len(v) for k, v in self.q.items()}))

    def emit(self, eng, fn, reads=(), writes=(), dsem=None):
        is_dma = dsem is not None
        waits = []
        for r in reads:
            for e2, tok in r.w.items():
                if not (e2 == eng and eng == "pe"):
                    waits.append(tok)
            if r.excl:
                for e2, tok in r.r.items():
                    if e2 != eng:
                        waits.append(tok)
        for w in writes:
            for e2, tok in w.w.items():
                if e2 != eng or eng != "pe":
                    waits.append(tok)
            for e2, tok in w.r.items():
                if e2 != eng or eng != "pe":
                    waits.append(tok)
        ins = Instr(eng, fn, waits, is_dma, dsem)
        for t in waits:
            t.awaited = True
        for r in reads:
            r.r[eng] = ins
        for w in writes:
            w.w[eng] = ins; w.r = {}
        self.q[eng].append(ins)
        return ins


def build_program(debug_stage=None, n_tiles=NT):
    nc = bass.Bass("TRN2", target_bir_lowering=False)
    P = Prog()

    def dram_in(name, shape):
        return nc.dram_tensor(name, list(shape), F32, kind="ExternalInput").ap()

    x_d = dram_in("x", (HALO + TOK_CORE, D))
    wup_d = [dram_in("w_up1", (D, 2 * DFF)), dram_in("w_up2", (D, 2 * DFF))]
    wdn_d = [dram_in("w_dn1", (DFF, D)), dram_in("w_dn2", (DFF, D))]
    win_d = dram_in("w_in", (D, 3840))
    wau_d = dram_in("w_au", (D, D))
    wpu_d = dram_in("w_pu", (512, D))
    wo_d = dram_in("w_o", (D, D))
    wmix_d = dram_in("w_mix", (4, 128, 128))
    emask_d = dram_in("emask", (128, 8 * 2 * 256))
    emlo_d = dram_in("emlo", (128, 8 * 256))
    ident_d = dram_in("ident", (128, 128))
    gcols_d = dram_in("gcols", (128, 24))
    gfin_d = dram_in("gfin", (128, D))
    sinkc_d = dram_in("sinkc", (128, 8))
    pscale_d = dram_in("pscale", (128, 4))
    pinv_d = dram_in("pinv", (128, 64))
    oneh_d = dram_in("oneh", (128, 64))
    y_d = nc.dram_tensor("y", [TOK_CORE, D], F32, kind="ExternalOutput").ap()

    with ExitStack() as es:
        E = es.enter_context

        def sb(name, shape, dt):
            return E(nc.sbuf_tensor(name, list(shape), dt))

        h = sb("h", (128, 9, D), F32)
        xs = sb("xs", (128, 2, D), BF16)
        xnT = sb("xnT", (128, NKC, NCOL), BF16)
        R1 = sb("R1", (128, 8, NCOL), BF16)
        R2 = sb("R2", (128, 4, NCOL), F32)
        pooledT = sb("pooledT", (128, 4, TT), BF16)
        mixedT = sb("mixedT", (128, 4, TT), BF16)
        KT = sb("KT", (128, 2, NCOL), BF16)
        V = sb("V", (128, 9, 2, 64), BF16)
        qT = sb("qT", (128, 2, TT), BF16)
        emask = sb("emask_s", (128, 8, 2, 256), BF16)
        emlo = sb("emlo_s", (128, 8, 256), BF16)
        T32 = sb("T32", (128, 4, 512), F32)
        PT = sb("PT", (128, 6, 512), BF16)
        junk = sb("junk", (128, D), BF16)
        dent = sb("dent", (128, 2, 256), F32)
        ident = sb("ident_s", (128, 128), BF16)
        ones = sb("ones", (128, 64), BF16)
        oneh = sb("oneh_s", (128, 64), BF16)
        gcols = sb("gcols_s", (128, 3, 8), F32)
        gfin = sb("gfin_s", (128, D), F32)
        esink = sb("esink", (128, 8), F32)
        pscale = sb("pscale_s", (128, 4), F32)
        pinv = sb("pinv_s", (128, 4, 16), F32)
        wmix = sb("wmix_s", (128, 4, 128), BF16)
        zcarry = sb("zcarry", (128, 4, 16), F32)
        stat = sb("stat", (128, 9, 4), F32)
        ring = sb("ring", (128, NS, SLOT_ELEMS), BF16)
        ps = E(nc.psum_tensor("ps", [128, 8, 512], F32))

        mergedT = R2[:].bitcast(BF16).rearrange("p a (b c) -> p (a b) c", b=2)

        r_h = [Res() for _ in range(9)]
        r_xs = [Res() for _ in range(2)]
        r_xnT = [Res() for _ in range(9)]
        r_hid = [[[Res() for _ in range(9)] for _ in range(4)] for _ in range(2)]
        r_attnT = [[Res() for _ in range(9)] for _ in range(8)]
        r_zb = [Res() for _ in range(4)]
        r_merged = [[Res() for _ in range(9)] for _ in range(8)]
        r_pooled = [Res() for _ in range(4)]
        r_mixed = [[Res(), Res()] for _ in range(4)]
        r_KT = [Res() for _ in range(9)]
        r_V = [Res() for _ in range(9)]
        r_qT = [Res() for _ in range(2)]
        r_T32 = [Res() for _ in range(4)]
        r_junk = Res()
        r_PT = [Res() for _ in range(6)]
        r_dent = [Res() for _ in range(2)]
        r_const = Res()
        r_stat = [Res() for _ in range(9)]
        r_zcarry = Res()
        r_bank = [Res(excl=True) for _ in range(8)]
        r_ring = [Res() for _ in range(NS)]

        n_eng_sems = 8
        eng_sems = {e: [E(nc.semaphore(f"s_{e}{k}")) for k in range(n_eng_sems)] for e in ("pe", "act", "dve")}
        ring_sems = [E(nc.semaphore(f"s_ring{k}")) for k in range(NS)]
        h_sems = [E(nc.semaphore(f"s_h{k}")) for k in range(9)]
        const_sems = {"pool": E(nc.semaphore("s_const_pool")), "sp": E(nc.semaphore("s_const_sp"))}
        dsem_count = {}

        def subs_of(blocks):
            out = []
            blocks = list(blocks)
            if blocks[0] == 0:
                out.append((0, 128, [0])); blocks = blocks[1:]
            for k in range(0, len(blocks), 4):
                bl = blocks[k:k + 4]
                out.append((bl[0] * 128, len(bl) * 128, bl))
            return out

        ring_pos = [0]

        def ring_load(dmas):
            s = ring_pos[0] % NS
            ring_pos[0] += 1
            slot = ring[:, s, :]
            for dst_fn, src in dmas:
                dst = dst_fn(slot)
                P.emit("pool", (lambda e, dst=dst, src=src: e.dma_start(out=dst, in_=src)),
                       writes=[r_ring[s]], dsem=ring_sems[s])
            return slot, r_ring[s]

        def wview(slot, nk, ncols):
            return slot[:, 0:nk * ncols].rearrange("p (k n) -> p k n", k=nk)

        def load_cols(src2d, c0, ncols, nk=NKC):
            src = src2d.rearrange("(k p) n -> p k n", p=128)[:, :, c0:c0 + ncols]
            slot, res = ring_load([(lambda sl: wview(sl, nk, ncols), src)])
            return wview(slot, nk, ncols), res

        t32_pos = [0]

        def t32_next():
            k = t32_pos[0] % 4
            t32_pos[0] += 1
            return T32[:, k, :], r_T32[k]

        def mm(out, lhsT, rhs, start, stop, reads, bank):
            P.emit("pe", (lambda e: e.matmul(out, lhsT=lhsT, rhs=rhs, start=start, stop=stop)),
                   reads=reads, writes=[r_bank[bank]])

        def const_dma(eng, dst, src):
            ins = P.emit(eng, (lambda e: e.dma_start(out=dst, in_=src)), dsem=const_sems[eng])
            r_const.w[eng] = ins

        const_dma("pool", emask[:].rearrange("p a b c -> p (a b c)"), emask_d[:, :])
        const_dma("pool", emlo[:].rearrange("p a c -> p (a c)"), emlo_d[:, :])
        const_dma("pool", ident[:], ident_d[:, :])
        const_dma("pool", oneh[:], oneh_d[:, :])
        const_dma("pool", wmix[:], wmix_d.rearrange("g c d -> c g d"))
        const_dma("sp", gcols[:].rearrange("p a b -> p (a b)"), gcols_d[:, :])
        const_dma("sp", gfin[:], gfin_d[:, :])
        const_dma("sp", esink[:], sinkc_d[:, :])
        const_dma("sp", pscale[:], pscale_d[:, :])
        const_dma("sp", pinv[:].rearrange("p a b -> p (a b)"), pinv_d[:, :])
        ins_ones = P.emit("dve", lambda e: e.memset(ones[:], 1.0))
        r_const.w["dve"] = ins_ones
        P.emit("dve", lambda e: e.memset(stat[:, :, 3], EPS), writes=r_stat)
        P.emit("dve", lambda e: e.memset(KT[:].rearrange("p a b -> p (a b)"), 0.0), writes=r_KT)
        P.emit("act", lambda e: e.activation(out=esink[:], in_=esink[:], func=AF.Exp), reads=[r_const], writes=[r_const])

        tp_bank = [0]
        NLAG = 3

        class NormPipe:
            def __init__(self, kind, out_rows=None, after_store=None):
                self.kind = kind; self.out_rows = out_rows; self.after_store = after_store
                self.pending = []; self.staged = []

            def pre(self):
                if len(self.pending) >= NLAG:
                    b = self.pending.pop(0)
                    self.staged.append((b, self._apply_a(b)))

            def step(self, b):
                self.pre()
                self.post(b)

            def post(self, b):
                while self.staged:
                    self._apply_b(*self.staged.pop(0))
                P.emit("act", (lambda e, b=b: e.activation(out=junk[:], in_=h[:, b, :], func=AF.Square,
                                                           scale=1.0 / 32.0, accum_out=stat[:, b, 0:1])),
                       reads=[r_h[b]], writes=[r_junk, r_stat[b]])
                P.emit("act", (lambda e, b=b: e.activation(out=stat[:, b, 1:2], in_=stat[:, b, 0:1], func=AF.Sqrt,
                                                           bias=stat[:, b, 3:4], scale=1.0)),
                       reads=[r_stat[b]], writes=[r_stat[b]])
                self.pending.append(b)

            def flush(self):
                while self.pending or self.staged:
                    while self.staged:
                        self._apply_b(*self.staged.pop(0))
                    if self.pending:
                        b = self.pending.pop(0)
                        self.staged.append((b, self._apply_a(b)))

            def _apply_a(self, b):
                P.emit("dve", (lambda e, b=b: e.reciprocal(out=stat[:, b, 2:3], in_=stat[:, b, 1:2])),
                       reads=[r_stat[b]], writes=[r_stat[b]])
                if self.kind == 3:
                    return None
                xb = tp_bank[0] % 2
                tp_bank[0] += 1
                P.emit("act", (lambda e, b=b, xb=xb: e.activation(out=xs[:, xb, :], in_=h[:, b, :], func=AF.Copy,
                                                                   scale=stat[:, b, 2:3])),
                       reads=[r_stat[b], r_h[b]], writes=[r_xs[xb]])
                return xb

            def _apply_b(self, b, xb):
                kind = self.kind
                if kind == 3:
                    P.emit("dve", (lambda e, b=b: e.scalar_tensor_tensor(out=h[:, b, :], in0=h[:, b, :], scalar=stat[:, b, 2:3],
                                                                         in1=gfin[:], op0=ALU.mult, op1=ALU.mult)),
                           reads=[r_stat[b], r_const, r_h[b]], writes=[r_h[b]])
                    row = self.out_rows(b)
                    P.emit("sp", (lambda e, b=b, row=row: e.dma_start(out=y_d[row:row + 128, :], in_=h[:, b, :])),
                           reads=[r_h[b]], dsem=h_sems[b])
                    if self.after_store is not None:
                        self.after_store(b)
                    return
                bank = xb
                pview = ps[:, bank, :].bitcast(BF16)
                for c in range(NKC):
                    P.emit("pe", (lambda e, c=c, xb=xb, pview=pview: e.transpose(out=pview[:, c * 128:(c + 1) * 128],
                                                                                 in_=xs[:, xb, c * 128:(c + 1) * 128],
                                                                                 identity=ident[:])),
                           reads=[r_xs[xb], r_const], writes=[r_bank[bank]])
                P.emit("dve", (lambda e, b=b, pview=pview, kind=kind: e.tensor_tensor(
                    out=xnT[:, :, b * 128:(b + 1) * 128],
                    in0=pview.rearrange("p (c t) -> p c t", c=NKC),
                    in1=gcols[:, kind, :].unsqueeze(2).to_broadcast([128, NKC, 128]), op=ALU.mult)),
                       reads=[r_bank[bank], r_const], writes=[r_xnT[b]])

        def norm(kind, blocks):
            np_ = NormPipe(kind)
            for b in blocks:
                np_.step(b)
            np_.flush()

        def ffn(which, blocks, pipe=None, flush_pipe=None):
            subs = subs_of(blocks)
            wup = wup_d[which]
            wdn = wdn_d[which]
            units = [0]
            dn_units = [0]

            def up(gi):
                j0, nj = FF_GROUPS[gi]
                wa, ra = load_cols(wup, j0 * 128, nj * 128)
                wb, rb = load_cols(wup, DFF + j0 * 128, nj * 128)
                buf = gi % 2
                order = [(j, sub) for j in range(nj) for sub in subs]
                if gi == 0 and flush_pipe is not None:
                    early = [(j, sub) for sub in subs for j in range(nj) if max(sub[2]) <= 5]
                    late = [(j, sub) for sub in subs for j in range(nj) if max(sub[2]) > 5]
                    order = early + [None] + late
                for item in order:
                    if item is None:
                        flush_pipe.flush()
                        continue
                    j, (c0, n, bl) = item
                    if True:
                        u = units[0]; units[0] += 1
                        ba = u % 2; bb = 2 + u % 2
                        xr = [r_xnT[b] for b in bl]
                        for kc in range(NKC):
                            mm(ps[:, ba, 0:n], wa[:, kc, j * 128:(j + 1) * 128], xnT[:, kc, c0:c0 + n], kc == 0, kc == NKC - 1,
                               [ra] + xr, ba)
                        for kc in range(NKC):
                            mm(ps[:, bb, 0:n], wb[:, kc, j * 128:(j + 1) * 128], xnT[:, kc, c0:c0 + n], kc == 0, kc == NKC - 1,
                               [rb] + xr, bb)
                        tv, tr = t32_next()
                        P.emit("act", (lambda e, tv=tv, ba=ba, n=n: e.activation(out=tv[:, 0:n], in_=ps[:, ba, 0:n], func=AF.Silu)),
                               reads=[r_bank[ba]], writes=[tr])
                        P.emit("dve", (lambda e, tv=tv, bb=bb, n=n, buf=buf, j=j, c0=c0: e.tensor_tensor(
                            out=R1[:, buf * 4 + j, c0:c0 + n], in0=ps[:, bb, 0:n], in1=tv[:, 0:n], op=ALU.mult)),
                               reads=[r_bank[bb], tr], writes=[r_hid[buf][j][b] for b in bl])

            def down(gis):
                last = gis[-1] == len(FF_GROUPS) - 1
                pieces = []
                for gi in gis:
                    j0, nj = FF_GROUPS[gi]
                    src = wdn.rearrange("(k p) n -> p k n", p=128)[:, j0:j0 + nj, :]
                    slot, rd = ring_load([(lambda sl, nj=nj: wview(sl, nj, D), src)])
                    pieces.append((gi % 2, nj, wview(slot, nj, D), rd))
                ntot = sum(p[1] for p in pieces)
                for b in blocks:
                    if last and pipe is not None:
                        pipe.pre()
                    v = dn_units[0]; dn_units[0] += 1
                    b0 = 4 + 2 * (v % 2)
                    k = 0
                    for (buf, nj, wd, rd) in pieces:
                        for j in range(nj):
                            for hf in range(2):
                                mm(ps[:, b0 + hf, :], R1[:, buf * 4 + j, b * 128:(b + 1) * 128], wd[:, j, hf * 512:(hf + 1) * 512],
                                   k == 0, k == ntot - 1, [rd, r_hid[buf][j][b]], b0 + hf)
                            k += 1
                    for hf in range(2):
                        P.emit("dve", (lambda e, b=b, b0=b0, hf=hf: e.scalar_tensor_tensor(
                            out=h[:, b, hf * 512:(hf + 1) * 512], in0=ps[:, b0 + hf, :], scalar=0.5, in1=h[:, b, hf * 512:(hf + 1) * 512],
                            op0=ALU.mult, op1=ALU.add)),
                               reads=[r_bank[b0 + hf], r_h[b]], writes=[r_h[b]])
                    if last and pipe is not None:
                        pipe.post(b)

            ng = len(FF_GROUPS)
            up(0)
            for gi in range(1, ng):
                up(gi)
                if gi - 1 < ng - 2:
                    down([gi - 1])
            down([ng - 2, ng - 1])

        def mixer(ti, pipe=None):
            blocks_all = list(range(0, 9)) if ti == 0 else list(range(1, 9))
            main = list(range(1, 9))
            subs_all = subs_of(blocks_all)
            subs_main = subs_of(main)
            bank_rr = [0]

            def nb():
                k = bank_rr[0] % 4
                bank_rr[0] += 1
                return k

            P.mark(f"t{ti}.kvz")
            if ti > 0:
                P.emit("act", lambda e: e.activation(out=KT[:, :, 0:128], in_=KT[:, :, 1024:1152], func=AF.Copy),
                       reads=[r_KT[8]], writes=[r_KT[0]])
                P.emit("act", lambda e: e.activation(out=V[:, 0].rearrange("p a b -> p (a b)"),
                                                     in_=V[:, 8].rearrange("p a b -> p (a b)"), func=AF.Copy),
                       reads=[r_V[8]], writes=[r_V[0]])
            wsrc = win_d.rearrange("(k p) n -> p k n", p=128)
            dm = []
            for kvh in range(2):
                for dup in range(2):
                    dm.append((lambda sl, kvh=kvh, dup=dup: sl[:, 0:NKC * 256].rearrange("p (k n) -> p k n", k=NKC)
                               [:, :, kvh * 128 + dup * 64: kvh * 128 + dup * 64 + 64],
                               wsrc[:, :, 1024 + kvh * 64: 1024 + kvh * 64 + 64]))
            dm.append((lambda sl: sl[:, NKC * 256:NKC * 384].rearrange("p (k n) -> p k n", k=NKC), wsrc[:, :, 1152:1280]))
            slot, rkv = ring_load(dm)
            wk = slot[:, 0:NKC * 256].rearrange("p (k n) -> p k n", k=NKC)
            wv = slot[:, NKC * 256:NKC * 384].rearrange("p (k n) -> p k n", k=NKC)
            def kproj(kvh):
                for (c0, n, bl) in subs_all:
                    bk = nb()
                    for kc in range(NKC):
                        mm(ps[:, bk, 0:n], wk[:, kc, kvh * 128:(kvh + 1) * 128], xnT[:, kc, c0:c0 + n], kc == 0, kc == NKC - 1,
                           [rkv] + [r_xnT[b] for b in bl], bk)
                    P.emit("act", (lambda e, bk=bk, n=n, c0=c0, kvh=kvh: e.activation(out=KT[:, kvh, c0:c0 + n], in_=ps[:, bk, 0:n],
                                                                                     func=AF.Copy)),
                           reads=[r_bank[bk]], writes=[r_KT[b] for b in bl])
            def vproj():
                for b in blocks_all:
                    bk = nb()
                    for kc in range(NKC):
                        mm(ps[:, bk, 0:128], xnT[:, kc, b * 128:(b + 1) * 128], wv[:, kc, :], kc == 0, kc == NKC - 1, [rkv, r_xnT[b]], bk)
                    P.emit("act", (lambda e, bk=bk, b=b: e.activation(out=V[:, b].rearrange("p a d -> p (a d)"), in_=ps[:, bk, 0:128], func=AF.Copy)),
                           reads=[r_bank[bk]], writes=[r_V[b]])
            wz, rz = load_cols(win_d, 1280, 512)
            A = R2[:, 2, :]
            B = R2[:, 3, :]
            def zproj(g):
                zb = R2[:, g % 2, :]
                rzb = r_zb[g % 2]
                w = POOL_W[g]
                if ti > 0:
                    P.emit("dve", (lambda e, zb=zb, g=g: e.tensor_copy(out=zb[:, 112:128], in_=zcarry[:, g, :])),
                           reads=[r_zcarry], writes=[rzb])
                for (c0, n, bl) in subs_all:
                    bk = nb()
                    for kc in range(NKC):
                        mm(ps[:, bk, 0:n], wz[:, kc, g * 128:(g + 1) * 128], xnT[:, kc, c0:c0 + n], kc == 0, kc == NKC - 1,
                           [rz] + [r_xnT[b] for b in bl], bk)
                    P.emit("act", (lambda e, bk=bk, n=n, c0=c0, zb=zb: e.activation(out=zb[:, c0:c0 + n], in_=ps[:, bk, 0:n], func=AF.Copy)),
                           reads=[r_bank[bk]], writes=[rzb])
                src = zb; rsrc = rzb
                nst = g + 1
                for k in range(1, nst + 1):
                    sh = 1 << (k - 1)
                    lo = 112 + (1 << k) - 1
                    dst, rdst = (A, r_zb[2]) if k % 2 == 1 else (B, r_zb[3])
                    P.emit("dve", (lambda e, dst=dst, src=src, lo=lo, sh=sh: e.tensor_tensor(
                        out=dst[:, lo:NCOL], in0=src[:, lo:NCOL], in1=src[:, lo - sh:NCOL - sh], op=ALU.add)),
                           reads=[rsrc], writes=[rdst])
                    src = dst; rsrc = rdst
                pb = g
                P.emit("dve", (lambda e, src=src, zb=zb, pb=pb, w=w: e.scalar_tensor_tensor(
                    out=pooledT[:, pb, 16:TT], in0=src[:, 144:NCOL], scalar=1.0 / w, in1=zb[:, 144:NCOL],
                    op0=ALU.mult, op1=ALU.subtract)),
                       reads=[rsrc, rzb], writes=[r_pooled[pb]])
                if ti == 0:
                    tv, tr = t32_next()
                    P.emit("dve", (lambda e, tv=tv, src=src, g=g: e.tensor_tensor(out=tv[:, 0:16], in0=src[:, 128:144], in1=pinv[:, g, :],
                                                                                 op=ALU.mult)),
                           reads=[rsrc, r_const], writes=[tr])
                    P.emit("dve", (lambda e, tv=tv, zb=zb, pb=pb: e.tensor_tensor(out=pooledT[:, pb, 0:16], in0=tv[:, 0:16],
                                                                                  in1=zb[:, 128:144], op=ALU.subtract)),
                           reads=[tr, rzb], writes=[r_pooled[pb]])
                else:
                    P.emit("dve", (lambda e, src=src, zb=zb, pb=pb, w=w: e.scalar_tensor_tensor(
                        out=pooledT[:, pb, 0:16], in0=src[:, 128:144], scalar=1.0 / w, in1=zb[:, 128:144],
                        op0=ALU.mult, op1=ALU.subtract)),
                           reads=[rsrc, rzb], writes=[r_pooled[pb]])
                P.emit("dve", (lambda e, zb=zb, g=g: e.tensor_copy(out=zcarry[:, g, :], in_=zb[:, 1136:1152])),
                       reads=[rzb], writes=[r_zcarry])

            def mixmm(g):
                pb = g
                for si in range(2):
                    bk = nb()
                    mm(ps[:, bk, :], wmix[:, g, :], pooledT[:, pb, si * 512:(si + 1) * 512], True, True, [r_const, r_pooled[pb]], bk)
                    P.emit("act", (lambda e, bk=bk, g=g, si=si: e.activation(out=mixedT[:, g, si * 512:(si + 1) * 512], in_=ps[:, bk, :],
                                                                            func=AF.Copy, scale=pscale[:, g:g + 1])),
                           reads=[r_bank[bk], r_const], writes=[r_mixed[g][si]])


            zproj(0)
            zproj(1)
            kproj(0)
            zproj(2)
            kproj(1)
            zproj(3)
            vproj()
            for g in range(4):
                mixmm(g)

            wq = [None, None]
            rq = [None, None]
            set_ctr = [0]

            def qproj(c):
                if c % 4 == 0:
                    wq[0], rq[0] = load_cols(win_d, (c // 4) * 512, 512)
                qb = c % 2
                sset = set_ctr[0] % 3
                set_ctr[0] += 1
                for si in range(2):
                    bq = sset + 3 * si
                    for kc in range(NKC):
                        mm(ps[:, bq, :], wq[0][:, kc, (c % 4) * 128:(c % 4 + 1) * 128], xnT[:, kc, 128 + si * 512:128 + (si + 1) * 512],
                           kc == 0, kc == NKC - 1, [rq[0]] + [r_xnT[b] for b in range(1 + si * 4, 5 + si * 4)], bq)
                for si in range(2):
                    bq = sset + 3 * si
                    P.emit("act", (lambda e, qb=qb, si=si, bq=bq: e.activation(out=qT[:, qb, si * 512:(si + 1) * 512], in_=ps[:, bq, :], func=AF.Copy)),
                           reads=[r_bank[bq]], writes=[r_qT[qb]])

            att_u = [0]

            def qk(c, up):
                u = att_u[0]
                kvh = c // 4
                qb = c % 2
                sset = set_ctr[0] % 3
                set_ctr[0] += 1
                b = 1 + 2 * up
                for seg in range(4):
                    qblk = b + seg // 2
                    kblk = qblk - 1 + seg % 2
                    for half in range(2):
                        bank = sset + 3 * half
                        p0 = half * 64
                        mm(ps[:, bank, seg * 128:(seg + 1) * 128], KT[p0:p0 + 64, kvh, kblk * 128:(kblk + 1) * 128],
                           qT[p0:p0 + 64, qb, (qblk - 1) * 128:qblk * 128], True, False, [r_KT[kblk], r_qT[qb]], bank)
                    for half in range(2):
                        bank = sset + 3 * half
                        mm(ps[:, bank, seg * 128:(seg + 1) * 128], ident[:], emask[:, c, half, (seg % 2) * 128:(seg % 2 + 1) * 128],
                           False, half == 1, [r_const], bank)
                        if half == 0:
                            mm(ps[:, bank, seg * 128:(seg + 1) * 128], ident[:], emlo[:, c, (seg % 2) * 128:(seg % 2 + 1) * 128],
                               False, True, [r_const], bank)
                for half in range(2):
                    bank = sset + 3 * half
                    pk = (u % 3) * 2 + half
                    P.emit("act", (lambda e, pk=pk, bank=bank: e.activation(out=PT[:, pk, :], in_=ps[:, bank, :], func=AF.Exp, scale=0.125)),
                           reads=[r_bank[bank]], writes=[r_PT[pk]])
                att_u[0] += 1
                return (u, c, up)

            def pv(tok):
                u, c, up = tok
                kvh = c // 4
                par = u % 2
                b = 1 + 2 * up
                bank = 6 + par
                for what in range(2):
                    for qi in range(2):
                        qblk = b + qi
                        col = what * 256 + qi * 128
                        for kk in range(2):
                            kblk = qblk - 1 + kk
                            seg = qi * 2 + kk
                            for half in range(2):
                                pk = (u % 3) * 2 + half
                                if what == 0:
                                    lhsT = V[:, kblk, kvh, :]
                                    rd = [r_V[kblk], r_PT[pk]]
                                else:
                                    lhsT = oneh[:] if (ti == 0 and kblk == 0) else ones[:]
                                    rd = [r_const, r_PT[pk]]
                                mm(ps[half * 64:(half + 1) * 64, bank, col:col + 128], lhsT, PT[:, pk, seg * 128:(seg + 1) * 128],
                                   kk == 0, kk == 1, rd, bank)
                dk = par
                P.emit("dve", (lambda e, dk=dk, bank=bank, c=c: e.tensor_scalar_add(out=dent[:, dk, :], in0=ps[:, bank, 256:512],
                                                                                   scalar1=esink[:, c:c + 1])),
                       reads=[r_bank[bank], r_const], writes=[r_dent[dk]])
                P.emit("dve", (lambda e, dk=dk: e.reciprocal(out=dent[:, dk, :], in_=dent[:, dk, :])),
                       reads=[r_dent[dk]], writes=[r_dent[dk]])
                P.emit("dve", (lambda e, dk=dk, bank=bank, c=c, b=b: e.tensor_tensor(out=R1[:, c, b * 128:(b + 2) * 128], in0=ps[:, bank, 0:256],
                                                                                    in1=dent[:, dk, :], op=ALU.mult)),
                       reads=[r_bank[bank], r_dent[dk]], writes=[r_attnT[c][b], r_attnT[c][b + 1]])

            P.mark(f"t{ti}.attn")
            qproj(0)
            seq = [(c, up) for c in range(8) for up in range(4)]
            toks = []
            for n, (c, up) in enumerate(seq):
                if up == 2 and c < 7:
                    qproj(c + 1)
                toks.append(qk(c, up))
                if n >= 2:
                    pv(toks[n - 2])
            pv(toks[-2])
            pv(toks[-1])

            P.mark(f"t{ti}.m4")
            for mg in range(2):
                wau, rau = load_cols(wau_d, mg * 512, 512)
                wga, rga = load_cols(win_d, 1792 + mg * 512, 512)
                wpu, rpu = load_cols(wpu_d, mg * 512, 512, nk=4)
                wgp, rgp = load_cols(win_d, 2816 + mg * 512, 512)
                for mi in range(4):
                    m = mg * 4 + mi
                    ms = slice(mi * 128, (mi + 1) * 128)
                    for si, (c0, n, bl) in enumerate(subs_main):
                        par = (mi * 2 + si) % 2
                        ba, bga, bp, bgp = (par * 4 + k for k in range(4))
                        xr = [r_xnT[b] for b in bl]
                        for kc in range(NKC):
                            mm(ps[:, ba, :], wau[:, kc, ms], R1[:, kc, c0:c0 + n], kc == 0, kc == NKC - 1, [rau] + [r_attnT[kc][b] for b in bl], ba)
                        for kc in range(NKC):
                            mm(ps[:, bga, :], wga[:, kc, ms], xnT[:, kc, c0:c0 + n], kc == 0, kc == NKC - 1, [rga] + xr, bga)
                        for kc in range(4):
                            mm(ps[:, bp, :], wpu[:, kc, ms], mixedT[:, kc, si * 512:(si + 1) * 512], kc == 0, kc == 3, [rpu, r_mixed[kc][si]], bp)
                        for kc in range(NKC):
                            mm(ps[:, bgp, :], wgp[:, kc, ms], xnT[:, kc, c0:c0 + n], kc == 0, kc == NKC - 1, [rgp] + xr, bgp)
                        t1v, t1r = t32_next()
                        t2v, t2r = t32_next()
                        P.emit("act", (lambda e, t1v=t1v, bga=bga: e.activation(out=t1v, in_=ps[:, bga, :], func=AF.Sigmoid)),
                               reads=[r_bank[bga]], writes=[t1r])
                        P.emit("act", (lambda e, t2v=t2v, bgp=bgp: e.activation(out=t2v, in_=ps[:, bgp, :], func=AF.Sigmoid)),
                               reads=[r_bank[bgp]], writes=[t2r])
                        P.emit("dve", (lambda e, t1v=t1v, ba=ba: e.tensor_tensor(out=t1v, in0=ps[:, ba, :], in1=t1v, op=ALU.mult)),
                               reads=[r_bank[ba], t1r], writes=[t1r])
                        P.emit("dve", (lambda e, t2v=t2v, bp=bp: e.tensor_tensor(out=t2v, in0=ps[:, bp, :], in1=t2v, op=ALU.mult)),
                               reads=[r_bank[bp], t2r], writes=[t2r])
                        P.emit("dve", (lambda e, t1v=t1v, t2v=t2v, m=m, c0=c0, n=n: e.tensor_tensor(out=mergedT[:, m, c0:c0 + n], in0=t1v, in1=t2v,
                                                                                                  op=ALU.add)),
                               reads=[t1r, t2r], writes=[r_merged[m][b] for b in bl])
            P.mark(f"t{ti}.m5")
            wo0, ro0 = load_cols(wo_d, 0, 512)
            wo1, ro1 = load_cols(wo_d, 512, 512)
            for bi, b in enumerate(main):
                if pipe is not None:
                    pipe.pre()
                b0 = 4 + 2 * (bi % 2)
                for hf, (wo, ro) in enumerate(((wo0, ro0), (wo1, ro1))):
                    for kc in range(NKC):
                        mm(ps[:, b0 + hf, :], mergedT[:, kc, b * 128:(b + 1) * 128], wo[:, kc, :], kc == 0, kc == NKC - 1,
                           [ro, r_merged[kc][b]], b0 + hf)
                for hf in range(2):
                    P.emit("dve", (lambda e, b=b, b0=b0, hf=hf: e.tensor_tensor(out=h[:, b, hf * 512:(hf + 1) * 512], in0=ps[:, b0 + hf, :],
                                                                               in1=h[:, b, hf * 512:(hf + 1) * 512], op=ALU.add)),
                           reads=[r_bank[b0 + hf], r_h[b]], writes=[r_h[b]])
                if pipe is not None:
                    pipe.post(b)

        def dump_h(ti):
            for b in range(1, 9):
                row = ti * TT + (b - 1) * 128
                P.emit("sp", (lambda e, b=b, row=row: e.dma_start(out=y_d[row:row + 128, :], in_=h[:, b, :])),
                       reads=[r_h[b]], dsem=h_sems[b])

        def load_x(ti, b):
            row = ti * TT + b * 128
            P.emit("sp", (lambda e, b=b, row=row: e.dma_start(out=h[:, b, :], in_=x_d[row:row + 128, :])),
                   writes=[r_h[b]], dsem=h_sems[b])

        n0 = None
        xq = []
        XLAG = 3
        for ti in range(n_tiles):
            blocks_all = list(range(0, 9)) if ti == 0 else list(range(1, 9))
            main = list(range(1, 9))
            P.mark(f"t{ti}.norm0")
            if n0 is None:
                for b in blocks_all:
                    load_x(ti, b)
                norm(0, blocks_all)
            else:
                while xq:
                    n0.step(xq.pop(0))
            P.mark(f"t{ti}.ffn1")
            if debug_stage == 1:
                ffn(0, blocks_all)
                dump_h(ti); n0 = None; continue
            n1 = NormPipe(1)
            ffn(0, blocks_all, pipe=n1, flush_pipe=n0)
            P.mark(f"t{ti}.norm1")
            n1.flush()
            n2 = NormPipe(2)
            mixer(ti, pipe=(None if debug_stage == 2 else n2))
            if debug_stage == 2:
                dump_h(ti); n0 = None; continue
            P.mark(f"t{ti}.norm2")
            P.mark(f"t{ti}.ffn2")
            if ti + 1 < n_tiles:
                n0 = NormPipe(0)

                def after_store(b, ti=ti, n0=n0):
                    load_x(ti + 1, b)
                    xq.append(b)
                    if len(xq) > XLAG:
                        n0.step(xq.pop(0))
            else:
                n0 = None
                after_store = None
            n3 = NormPipe(3, out_rows=(lambda b, ti=ti: ti * TT + (b - 1) * 128), after_store=after_store)
            ffn(1, main, pipe=n3, flush_pipe=n2)
            n3.flush()

        for eng in ("pe", "act", "dve"):
            n = 0
            for ins in P.q[eng]:
                if ins.awaited:
                    ins.sem = eng_sems[eng][n // SEM_CH]
                    ins.val = n % SEM_CH + 1
                    n += 1
            assert n <= SEM_CH * n_eng_sems, (eng, n)
        for eng in ("pool", "sp"):
            n_const = sum(1 for ins in P.q[eng] if ins.dsem is const_sems[eng])
            for ins in P.q[eng]:
                k = id(ins.dsem)
                dsem_count[k] = dsem_count.get(k, 0) + 16
                ins.sem = ins.dsem
                ins.val = 16 * n_const if ins.dsem is const_sems[eng] else dsem_count[k]
        last_out = {}
        for ins in P.q["sp"]:
            last_out[id(ins.sem)] = ins

        block = E(nc.Block())

        def replay(engname):
            def run(e):
                waited = {}
                for ins in P.q[engname]:
                    for t in ins.waits:
                        k = id(t.sem)
                        if waited.get(k, 0) < t.val:
                            e.wait_ge(t.sem, t.val)
                            waited[k] = t.val
                    bi = ins.fn(e)
                    if ins.is_dma:
                        bi.then_inc(ins.sem, 16)
                    elif ins.awaited:
                        bi.then_inc(ins.sem, 1)
                if engname == "sp":
                    for ins in last_out.values():
                        e.wait_ge(ins.sem, ins.val)
            return run

        block.gpsimd(replay("pool"))
        block.sync(replay("sp"))
        block.scalar(replay("act"))
        block.vector(replay("dve"))
        block.tensor(replay("pe"))
    nc._marks = P.marks
    return nc


def _alibi_slopes():
    hh = np.arange(1, 17, dtype=np.float32)
    return (2.0 ** (-8.0 * hh / 16)).astype(np.float32)


def _bf16_round(a):
    u = np.ascontiguousarray(a, dtype=np.float32).view(np.uint32).astype(np.uint64)
    u = (u + 0x7FFF + ((u >> 16) & 1)) & 0xFFFF0000
    return u.astype(np.uint32).view(np.float32)


def _host_constants():
    sl = _alibi_slopes()
    k = np.arange(128)[:, None].astype(np.float32)
    q = np.arange(128)[None, :].astype(np.float32)
    em = np.zeros((128, 8, 2, 256), np.float32)
    for c in range(8):
        for half in range(2):
            s = sl[2 * c + half]
            dprev = 128.0 + q - k
            dcur = q - k
            em[:, c, half, 0:128] = np.where(dprev < 128, -8.0 * s * dprev, -240000.0)
            em[:, c, half, 128:256] = np.where(dcur >= 0, -8.0 * s * dcur, -240000.0)
    hi = _bf16_round(em)
    lo = np.where(em <= -200000.0, 0.0, _bf16_round(em - hi)).astype(np.float32)
    assert np.all(lo[:, :, 1, :] == 0.0)
    return hi.reshape(128, -1), np.ascontiguousarray(lo[:, :, 0, :]).reshape(128, -1), np.eye(128, dtype=np.float32)


_NC_CACHE = {}


def kernel(x, ffn1_norm, ffn1_w_up, ffn1_w_down, mix_norm, w_in, sinks, w_attn_up,
           pool_w_mix, pool_scale, w_pool_up, w_out, ffn2_norm, ffn2_w_up, ffn2_w_down,
           final_norm, _debug_stage=None, _n_tiles=NT):
    f = lambda a: np.ascontiguousarray(np.asarray(a, dtype=np.float32))
    x = f(x)
    Bn, S, _ = x.shape
    em, emlo, ident = _host_constants()
    gcols = np.stack([f(ffn1_norm)[0].reshape(8, 128).T, f(mix_norm)[0].reshape(8, 128).T,
                      f(ffn2_norm)[0].reshape(8, 128).T], axis=1).reshape(128, 24)
    gfin = np.ascontiguousarray(np.broadcast_to(f(final_norm)[None, :], (128, D)))
    sk = f(sinks)[0]
    sinkc = np.ascontiguousarray(np.repeat(sk.reshape(8, 2), 64, axis=1).T)
    pscale = np.ascontiguousarray(f(pool_scale)[0].reshape(4, 128).T)
    shared = {
        "w_up1": f(ffn1_w_up)[0], "w_dn1": f(ffn1_w_down)[0], "w_up2": f(ffn2_w_up)[0], "w_dn2": f(ffn2_w_down)[0],
        "w_in": f(w_in)[0], "w_au": f(w_attn_up)[0], "w_pu": f(w_pool_up)[0], "w_o": f(w_out)[0],
        "w_mix": f(pool_w_mix)[0], "emask": em, "emlo": emlo, "ident": ident, "gcols": np.ascontiguousarray(gcols), "gfin": gfin,
        "sinkc": sinkc, "pscale": pscale,
    }
    in_maps = []
    for core in range(8):
        b, half = core // 2, core % 2
        xc = np.zeros((HALO + TOK_CORE, D), np.float32)
        if half == 0:
            xc[HALO:] = x[b, 0:TOK_CORE]
        else:
            xc[:] = x[b, TOK_CORE - HALO:S]
        pinv = np.zeros((128, 4, 16), np.float32)
        for g, w in enumerate(POOL_W):
            t = np.arange(16)
            pinv[:, g, :] = (1.0 / np.minimum(t + 1, w)) if half == 0 else (1.0 / w)
        oneh = np.full((128, 64), 1.0 if half == 1 else 0.0, np.float32)
        m = dict(shared)
        m["x"] = xc
        m["pinv"] = pinv.reshape(128, 64)
        m["oneh"] = oneh
        in_maps.append(m)
    key = (_debug_stage, _n_tiles)
    if key not in _NC_CACHE:
        _NC_CACHE[key] = build_program(_debug_stage, _n_tiles)
    nc = _NC_CACHE[key]
    res = run_bass_kernel_spmd(nc, in_maps, core_ids=list(range(8)))
    out = np.empty((Bn, S, D), np.float32)
    for core in range(8):
        b, half = core // 2, core % 2
        out[b, half * TOK_CORE:(half + 1) * TOK_CORE] = res.results[core]["y"]
    return out
```

```python
import numpy as np
from contextlib import ExitStack
import concourse.bass as bass
import concourse.mybir as mybir
from concourse.bass_utils import run_bass_kernel_spmd

F32 = mybir.dt.float32
BF16 = mybir.dt.bfloat16
AF = mybir.ActivationFunctionType
ALU = mybir.AluOpType

D = 1024
DFF = 2816
NKC = 8
NT = 4
TT = 1024
NCOL = 1152
TOK_CORE = 4096
HALO = 128
NS = 6
SLOT_ELEMS = 4096
SEM_CH = 3000
POOL_W = (2, 4, 8, 16)
EPS = 1e-6
FF_GROUPS = [(0, 2), (2, 4), (6, 4), (10, 4), (14, 4), (18, 4)]


class Instr:
    __slots__ = ("eng", "fn", "waits", "sem", "val", "awaited", "is_dma", "dsem")

    def __init__(self, eng, fn, waits, is_dma=False, dsem=None):
        self.eng = eng; self.fn = fn; self.waits = waits
        self.sem = None; self.val = None
        self.awaited = False; self.is_dma = is_dma; self.dsem = dsem


class Res:
    __slots__ = ("w", "r", "excl")

    def __init__(self, excl=False):
        self.w = {}; self.r = {}; self.excl = excl


class Prog:
    def __init__(self):
        self.q = {"pe": [], "act": [], "dve": [], "pool": [], "sp": []}
        self.marks = []

    def mark(self, label):
        self.marks.append((label, {k: len(v) for k, v in self.q.items()}))

    def emit(self, eng, fn, reads=(), writes=(), dsem=None):
        is_dma = dsem is not None
        waits = []
        for r in reads:
            for e2, tok in r.w.items():
                if not (e2 == eng and eng == "pe"):
                    waits.append(tok)
            if r.excl:
                for e2, tok in r.r.items():
                    if e2 != eng:
                        waits.append(tok)
        for w in writes:
            for e2, tok in w.w.items():
                if e2 != eng or eng != "pe":
                    waits.append(tok)
            for e2, tok in w.r.items():
                if e2 != eng or eng != "pe":
                    waits.append(tok)
        ins = Instr(eng, fn, waits, is_dma, dsem)
        for t in waits:
            t.awaited = True
        for r in reads:
            r.r[eng] = ins
        for w in writes:
            w.w[eng] = ins; w.r = {}
        self.q[eng].append(ins)
        return ins


def build_program(debug_stage=None, n_tiles=NT):
    nc = bass.Bass("TRN2", target_bir_lowering=False)
    P = Prog()

    def dram_in(name, shape):
        return nc.dram_tensor(name, list(shape), F32, kind="ExternalInput").ap()

    x_d = dram_in("x", (HALO + TOK_CORE, D))
    wup_d = [dram_in("w_up1", (D, 2 * DFF)), dram_in("w_up2", (D, 2 * DFF))]
    wdn_d = [dram_in("w_dn1", (DFF, D)), dram_in("w_dn2", (DFF, D))]
    win_d = dram_in("w_in", (D, 3840))
    wau_d = dram_in("w_au", (D, D))
    wpu_d = dram_in("w_pu", (512, D))
    wo_d = dram_in("w_o", (D, D))
    wmix_d = dram_in("w_mix", (4, 128, 128))
    emask_d = dram_in("emask", (128, 8 * 2 * 256))
    emlo_d = dram_in("emlo", (128, 8 * 256))
    ident_d = dram_in("ident", (128, 128))
    gcols_d = dram_in("gcols", (128, 24))
    gfin_d = dram_in("gfin", (128, D))
    sinkc_d = dram_in("sinkc", (128, 8))
    pscale_d = dram_in("pscale", (128, 4))
    pinv_d = dram_in("pinv", (128, 64))
    oneh_d = dram_in("oneh", (128, 64))
    y_d = nc.dram_tensor("y", [TOK_CORE, D], F32, kind="ExternalOutput").ap()

    with ExitStack() as es:
        E = es.enter_context

        def sb(name, shape, dt):
            return E(nc.sbuf_tensor(name, list(shape), dt))

        h = sb("h", (128, 9, D), F32)
        xs = sb("xs", (128, 2, D), BF16)
        xnT = sb("xnT", (128, NKC, NCOL), BF16)
        R1 = sb("R1", (128, 8, NCOL), BF16)
        R2 = sb("R2", (128, 4, NCOL), F32)
        pooledT = sb("pooledT", (128, 4, TT), BF16)
        mixedT = sb("mixedT", (128, 4, TT), BF16)
        KT = sb("KT", (128, 2, NCOL), BF16)
        V = sb("V", (128, 9, 2, 64), BF16)
        qT = sb("qT", (128, 2, TT), BF16)
        emask = sb("emask_s", (128, 8, 2, 256), BF16)
        emlo = sb("emlo_s", (128, 8, 256), BF16)
        T32 = sb("T32", (128, 4, 512), F32)
        PT = sb("PT", (128, 6, 512), BF16)
        junk = sb("junk", (128, D), BF16)
        dent = sb("dent", (128, 2, 256), F32)
        ident = sb("ident_s", (128, 128), BF16)
        ones = sb("ones", (128, 64), BF16)
        oneh = sb("oneh_s", (128, 64), BF16)
        gcols = sb("gcols_s", (128, 3, 8), F32)
        gfin = sb("gfin_s", (128, D), F32)
        esink = sb("esink", (128, 8), F32)
        pscale = sb("pscale_s", (128, 4), F32)
        pinv = sb("pinv_s", (128, 4, 16), F32)
        wmix = sb("wmix_s", (128, 4, 128), BF16)
        zcarry = sb("zcarry", (128, 4, 16), F32)
        stat = sb("stat", (128, 9, 4), F32)
        ring = sb("ring", (128, NS, SLOT_ELEMS), BF16)
        ps = E(nc.psum_tensor("ps", [128, 8, 512], F32))

        mergedT = R2[:].bitcast(BF16).rearrange("p a (b c) -> p (a b) c", b=2)

        r_h = [Res() for _ in range(9)]
        r_xs = [Res() for _ in range(2)]
        r_xnT = [Res() for _ in range(9)]
        r_hid = [[[Res() for _ in range(9)] for _ in range(4)] for _ in range(2)]
        r_attnT = [[Res() for _ in range(9)] for _ in range(8)]
        r_zb = [Res() for _ in range(4)]
        r_merged = [[Res() for _ in range(9)] for _ in range(8)]
        r_pooled = [Res() for _ in range(4)]
        r_mixed = [[Res(), Res()] for _ in range(4)]
        r_KT = [Res() for _ in range(9)]
        r_V = [Res() for _ in range(9)]
        r_qT = [Res() for _ in range(2)]
        r_T32 = [Res() for _ in range(4)]
        r_junk = Res()
        r_PT = [Res() for _ in range(6)]
        r_dent = [Res() for _ in range(2)]
        r_const = Res()
        r_stat = [Res() for _ in range(9)]
        r_zcarry = Res()
        r_bank = [Res(excl=True) for _ in range(8)]
        r_ring = [Res() for _ in range(NS)]

        n_eng_sems = 8
        eng_sems = {e: [E(nc.semaphore(f"s_{e}{k}")) for k in range(n_eng_sems)] for e in ("pe", "act", "dve")}
        ring_sems = [E(nc.semaphore(f"s_ring{k}")) for k in range(NS)]
        h_sems = [E(nc.semaphore(f"s_h{k}")) for k in range(9)]
        const_sems = {"pool": E(nc.semaphore("s_const_pool")), "sp": E(nc.semaphore("s_const_sp"))}
        dsem_count = {}

        def subs_of(blocks):
            out = []
            blocks = list(blocks)
            if blocks[0] == 0:
                out.append((0, 128, [0])); blocks = blocks[1:]
            for k in range(0, len(blocks), 4):
                bl = blocks[k:k + 4]
                out.append((bl[0] * 128, len(bl) * 128, bl))
            return out

        ring_pos = [0]

        def ring_load(dmas):
            s = ring_pos[0] % NS
            ring_pos[0] += 1
            slot = ring[:, s, :]
            for dst_fn, src in dmas:
                dst = dst_fn(slot)
                P.emit("pool", (lambda e, dst=dst, src=src: e.dma_start(out=dst, in_=src)),
                       writes=[r_ring[s]], dsem=ring_sems[s])
            return slot, r_ring[s]

        def wview(slot, nk, ncols):
            return slot[:, 0:nk * ncols].rearrange("p (k n) -> p k n", k=nk)

        def load_cols(src2d, c0, ncols, nk=NKC):
            src = src2d.rearrange("(k p) n -> p k n", p=128)[:, :, c0:c0 + ncols]
            slot, res = ring_load([(lambda sl: wview(sl, nk, ncols), src)])
            return wview(slot, nk, ncols), res

        t32_pos = [0]

        def t32_next():
            k = t32_pos[0] % 4
            t32_pos[0] += 1
            return T32[:, k, :], r_T32[k]

        def mm(out, lhsT, rhs, start, stop, reads, bank):
            P.emit("pe", (lambda e: e.matmul(out, lhsT=lhsT, rhs=rhs, start=start, stop=stop)),
                   reads=reads, writes=[r_bank[bank]])

        def const_dma(eng, dst, src):
            ins = P.emit(eng, (lambda e: e.dma_start(out=dst, in_=src)), dsem=const_sems[eng])
            r_const.w[eng] = ins

        const_dma("pool", emask[:].rearrange("p a b c -> p (a b c)"), emask_d[:, :])
        const_dma("pool", emlo[:].rearrange("p a c -> p (a c)"), emlo_d[:, :])
        const_dma("pool", ident[:], ident_d[:, :])
        const_dma("pool", oneh[:], oneh_d[:, :])
        const_dma("pool", wmix[:], wmix_d.rearrange("g c d -> c g d"))
        const_dma("sp", gcols[:].rearrange("p a b -> p (a b)"), gcols_d[:, :])
        const_dma("sp", gfin[:], gfin_d[:, :])
        const_dma("sp", esink[:], sinkc_d[:, :])
        const_dma("sp", pscale[:], pscale_d[:, :])
        const_dma("sp", pinv[:].rearrange("p a b -> p (a b)"), pinv_d[:, :])
        ins_ones = P.emit("dve", lambda e: e.memset(ones[:], 1.0))
        r_const.w["dve"] = ins_ones
        P.emit("dve", lambda e: e.memset(stat[:, :, 3], EPS), writes=r_stat)
        P.emit("dve", lambda e: e.memset(KT[:].rearrange("p a b -> p (a b)"), 0.0), writes=r_KT)
        P.emit("act", lambda e: e.activation(out=esink[:], in_=esink[:], func=AF.Exp), reads=[r_const], writes=[r_const])

        tp_bank = [0]
        NLAG = 2

        class NormPipe:
            def __init__(self, kind, out_rows=None, after_store=None):
                self.kind = kind; self.out_rows = out_rows; self.after_store = after_store
                self.pending = []; self.staged = []

            def pre(self):
                if len(self.pending) >= NLAG:
                    b = self.pending.pop(0)
                    self.staged.append((b, self._apply_a(b)))

            def step(self, b):
                self.pre()
                self.post(b)

            def post(self, b):
                while self.staged:
                    self._apply_b(*self.staged.pop(0))
                P.emit("act", (lambda e, b=b: e.activation(out=junk[:], in_=h[:, b, :], func=AF.Square,
                                                           scale=1.0 / 32.0, accum_out=stat[:, b, 0:1])),
                       reads=[r_h[b]], writes=[r_junk, r_stat[b]])
                P.emit("act", (lambda e, b=b: e.activation(out=stat[:, b, 1:2], in_=stat[:, b, 0:1], func=AF.Sqrt,
                                                           bias=stat[:, b, 3:4], scale=1.0)),
                       reads=[r_stat[b]], writes=[r_stat[b]])
                self.pending.append(b)

            def flush(self):
                while self.pending or self.staged:
                    while self.staged:
                        self._apply_b(*self.staged.pop(0))
                    if self.pending:
                        b = self.pending.pop(0)
                        self.staged.append((b, self._apply_a(b)))

            def _apply_a(self, b):
                P.emit("dve", (lambda e, b=b: e.reciprocal(out=stat[:, b, 2:3], in_=stat[:, b, 1:2])),
                       reads=[r_stat[b]], writes=[r_stat[b]])
                if self.kind == 3:
                    return None
                xb = tp_bank[0] % 2
                tp_bank[0] += 1
                P.emit("act", (lambda e, b=b, xb=xb: e.activation(out=xs[:, xb, :], in_=h[:, b, :], func=AF.Copy,
                                                                   scale=stat[:, b, 2:3])),
                       reads=[r_stat[b], r_h[b]], writes=[r_xs[xb]])
                return xb

            def _apply_b(self, b, xb):
                kind = self.kind
                if kind == 3:
                    P.emit("dve", (lambda e, b=b: e.scalar_tensor_tensor(out=h[:, b, :], in0=h[:, b, :], scalar=stat[:, b, 2:3],
                                                                         in1=gfin[:], op0=ALU.mult, op1=ALU.mult)),
                           reads=[r_stat[b], r_const, r_h[b]], writes=[r_h[b]])
                    row = self.out_rows(b)
                    P.emit("sp", (lambda e, b=b, row=row: e.dma_start(out=y_d[row:row + 128, :], in_=h[:, b, :])),
                           reads=[r_h[b]], dsem=h_sems[b])
                    if self.after_store is not None:
                        self.after_store(b)
                    return
                bank = xb
                pview = ps[:, bank, :].bitcast(BF16)
                for c in range(NKC):
                    P.emit("pe", (lambda e, c=c, xb=xb, pview=pview: e.transpose(out=pview[:, c * 128:(c + 1) * 128],
                                                                                 in_=xs[:, xb, c * 128:(c + 1) * 128],
                                                                                 identity=ident[:])),
                           reads=[r_xs[xb], r_const], writes=[r_bank[bank]])
                P.emit("dve", (lambda e, b=b, pview=pview, kind=kind: e.tensor_tensor(
                    out=xnT[:, :, b * 128:(b + 1) * 128],
                    in0=pview.rearrange("p (c t) -> p c t", c=NKC),
                    in1=gcols[:, kind, :].unsqueeze(2).to_broadcast([128, NKC, 128]), op=ALU.mult)),
                       reads=[r_bank[bank], r_const], writes=[r_xnT[b]])

        def norm(kind, blocks):
            np_ = NormPipe(kind)
            for b in blocks:
                np_.step(b)
            np_.flush()

        def ffn(which, blocks, pipe=None):
            subs = subs_of(blocks)
            wup = wup_d[which]
            wdn = wdn_d[which]
            units = [0]
            dn_units = [0]

            def up(gi):
                j0, nj = FF_GROUPS[gi]
                wa, ra = load_cols(wup, j0 * 128, nj * 128)
                wb, rb = load_cols(wup, DFF + j0 * 128, nj * 128)
                buf = gi % 2
                for j in range(nj):
                    for (c0, n, bl) in subs:
                        u = units[0]; units[0] += 1
                        ba = u % 2; bb = 2 + u % 2
                        xr = [r_xnT[b] for b in bl]
                        for kc in range(NKC):
                            mm(ps[:, ba, 0:n], wa[:, kc, j * 128:(j + 1) * 128], xnT[:, kc, c0:c0 + n], kc == 0, kc == NKC - 1,
                               [ra] + xr, ba)
                        for kc in range(NKC):
                            mm(ps[:, bb, 0:n], wb[:, kc, j * 128:(j + 1) * 128], xnT[:, kc, c0:c0 + n], kc == 0, kc == NKC - 1,
                               [rb] + xr, bb)
                        tv, tr = t32_next()
                        P.emit("act", (lambda e, tv=tv, ba=ba, n=n: e.activation(out=tv[:, 0:n], in_=ps[:, ba, 0:n], func=AF.Silu)),
                               reads=[r_bank[ba]], writes=[tr])
                        P.emit("dve", (lambda e, tv=tv, bb=bb, n=n, buf=buf, j=j, c0=c0: e.tensor_tensor(
                            out=R1[:, buf * 4 + j, c0:c0 + n], in0=ps[:, bb, 0:n], in1=tv[:, 0:n], op=ALU.mult)),
                               reads=[r_bank[bb], tr], writes=[r_hid[buf][j][b] for b in bl])

            def down(gis):
                last = gis[-1] == len(FF_GROUPS) - 1
                pieces = []
                for gi in gis:
                    j0, nj = FF_GROUPS[gi]
                    src = wdn.rearrange("(k p) n -> p k n", p=128)[:, j0:j0 + nj, :]
                    slot, rd = ring_load([(lambda sl, nj=nj: wview(sl, nj, D), src)])
                    pieces.append((gi % 2, nj, wview(slot, nj, D), rd))
                ntot = sum(p[1] for p in pieces)
                for b in blocks:
                    if last and pipe is not None:
                        pipe.pre()
                    v = dn_units[0]; dn_units[0] += 1
                    b0 = 4 + 2 * (v % 2)
                    k = 0
                    for (buf, nj, wd, rd) in pieces:
                        for j in range(nj):
                            for hf in range(2):
                                mm(ps[:, b0 + hf, :], R1[:, buf * 4 + j, b * 128:(b + 1) * 128], wd[:, j, hf * 512:(hf + 1) * 512],
                                   k == 0, k == ntot - 1, [rd, r_hid[buf][j][b]], b0 + hf)
                            k += 1
                    for hf in range(2):
                        P.emit("dve", (lambda e, b=b, b0=b0, hf=hf: e.scalar_tensor_tensor(
                            out=h[:, b, hf * 512:(hf + 1) * 512], in0=ps[:, b0 + hf, :], scalar=0.5, in1=h[:, b, hf * 512:(hf + 1) * 512],
                            op0=ALU.mult, op1=ALU.add)),
                               reads=[r_bank[b0 + hf], r_h[b]], writes=[r_h[b]])
                    if last and pipe is not None:
                        pipe.post(b)

            ng = len(FF_GROUPS)
            up(0)
            for gi in range(1, ng):
                up(gi)
                if gi - 1 < ng - 2:
                    down([gi - 1])
            down([ng - 2, ng - 1])

        def mixer(ti, pipe=None):
            blocks_all = list(range(0, 9)) if ti == 0 else list(range(1, 9))
            main = list(range(1, 9))
            subs_all = subs_of(blocks_all)
            subs_main = subs_of(main)
            bank_rr = [0]

            def nb():
                k = bank_rr[0] % 4
                bank_rr[0] += 1
                return k

            P.mark(f"t{ti}.kvz")
            if ti > 0:
                P.emit("act", lambda e: e.activation(out=KT[:, :, 0:128], in_=KT[:, :, 1024:1152], func=AF.Copy),
                       reads=[r_KT[8]], writes=[r_KT[0]])
                P.emit("act", lambda e: e.activation(out=V[:, 0].rearrange("p a b -> p (a b)"),
                                                     in_=V[:, 8].rearrange("p a b -> p (a b)"), func=AF.Copy),
                       reads=[r_V[8]], writes=[r_V[0]])
            wsrc = win_d.rearrange("(k p) n -> p k n", p=128)
            dm = []
            for kvh in range(2):
                for dup in range(2):
                    dm.append((lambda sl, kvh=kvh, dup=dup: sl[:, 0:NKC * 256].rearrange("p (k n) -> p k n", k=NKC)
                               [:, :, kvh * 128 + dup * 64: kvh * 128 + dup * 64 + 64],
                               wsrc[:, :, 1024 + kvh * 64: 1024 + kvh * 64 + 64]))
            dm.append((lambda sl: sl[:, NKC * 256:NKC * 384].rearrange("p (k n) -> p k n", k=NKC), wsrc[:, :, 1152:1280]))
            slot, rkv = ring_load(dm)
            wk = slot[:, 0:NKC * 256].rearrange("p (k n) -> p k n", k=NKC)
            wv = slot[:, NKC * 256:NKC * 384].rearrange("p (k n) -> p k n", k=NKC)
            def kproj(kvh):
                for (c0, n, bl) in subs_all:
                    bk = nb()
                    for kc in range(NKC):
                        mm(ps[:, bk, 0:n], wk[:, kc, kvh * 128:(kvh + 1) * 128], xnT[:, kc, c0:c0 + n], kc == 0, kc == NKC - 1,
                           [rkv] + [r_xnT[b] for b in bl], bk)
                    P.emit("act", (lambda e, bk=bk, n=n, c0=c0, kvh=kvh: e.activation(out=KT[:, kvh, c0:c0 + n], in_=ps[:, bk, 0:n],
                                                                                     func=AF.Copy)),
                           reads=[r_bank[bk]], writes=[r_KT[b] for b in bl])
            def vproj():
                for b in blocks_all:
                    bk = nb()
                    for kc in range(NKC):
                        mm(ps[:, bk, 0:128], xnT[:, kc, b * 128:(b + 1) * 128], wv[:, kc, :], kc == 0, kc == NKC - 1, [rkv, r_xnT[b]], bk)
                    P.emit("act", (lambda e, bk=bk, b=b: e.activation(out=V[:, b].rearrange("p a d -> p (a d)"), in_=ps[:, bk, 0:128], func=AF.Copy)),
                           reads=[r_bank[bk]], writes=[r_V[b]])
            wz, rz = load_cols(win_d, 1280, 512)
            A = R2[:, 2, :]
            B = R2[:, 3, :]
            def zproj(g):
                zb = R2[:, g % 2, :]
                rzb = r_zb[g % 2]
                w = POOL_W[g]
                if ti > 0:
                    P.emit("dve", (lambda e, zb=zb, g=g: e.tensor_copy(out=zb[:, 112:128], in_=zcarry[:, g, :])),
                           reads=[r_zcarry], writes=[rzb])
                for (c0, n, bl) in subs_all:
                    bk = nb()
                    for kc in range(NKC):
                        mm(ps[:, bk, 0:n], wz[:, kc, g * 128:(g + 1) * 128], xnT[:, kc, c0:c0 + n], kc == 0, kc == NKC - 1,
                           [rz] + [r_xnT[b] for b in bl], bk)
                    P.emit("act", (lambda e, bk=bk, n=n, c0=c0, zb=zb: e.activation(out=zb[:, c0:c0 + n], in_=ps[:, bk, 0:n], func=AF.Copy)),
                           reads=[r_bank[bk]], writes=[rzb])
                src = zb; rsrc = rzb
                nst = g + 1
                for k in range(1, nst + 1):
                    sh = 1 << (k - 1)
                    lo = 112 + (1 << k) - 1
                    dst, rdst = (A, r_zb[2]) if k % 2 == 1 else (B, r_zb[3])
                    P.emit("dve", (lambda e, dst=dst, src=src, lo=lo, sh=sh: e.tensor_tensor(
                        out=dst[:, lo:NCOL], in0=src[:, lo:NCOL], in1=src[:, lo - sh:NCOL - sh], op=ALU.add)),
                           reads=[rsrc], writes=[rdst])
                    src = dst; rsrc = rdst
                pb = g
                P.emit("dve", (lambda e, src=src, zb=zb, pb=pb, w=w: e.scalar_tensor_tensor(
                    out=pooledT[:, pb, 16:TT], in0=src[:, 144:NCOL], scalar=1.0 / w, in1=zb[:, 144:NCOL],
                    op0=ALU.mult, op1=ALU.subtract)),
                       reads=[rsrc, rzb], writes=[r_pooled[pb]])
                if ti == 0:
                    tv, tr = t32_next()
                    P.emit("dve", (lambda e, tv=tv, src=src, g=g: e.tensor_tensor(out=tv[:, 0:16], in0=src[:, 128:144], in1=pinv[:, g, :],
                                                                                 op=ALU.mult)),
                           reads=[rsrc, r_const], writes=[tr])
                    P.emit("dve", (lambda e, tv=tv, zb=zb, pb=pb: e.tensor_tensor(out=pooledT[:, pb, 0:16], in0=tv[:, 0:16],
                                                                                  in1=zb[:, 128:144], op=ALU.subtract)),
                           reads=[tr, rzb], writes=[r_pooled[pb]])
                else:
                    P.emit("dve", (lambda e, src=src, zb=zb, pb=pb, w=w: e.scalar_tensor_tensor(
                        out=pooledT[:, pb, 0:16], in0=src[:, 128:144], scalar=1.0 / w, in1=zb[:, 128:144],
                        op0=ALU.mult, op1=ALU.subtract)),
                           reads=[rsrc, rzb], writes=[r_pooled[pb]])
                P.emit("dve", (lambda e, zb=zb, g=g: e.tensor_copy(out=zcarry[:, g, :], in_=zb[:, 1136:1152])),
                       reads=[rzb], writes=[r_zcarry])

            def mixmm(g):
                pb = g
                for si in range(2):
                    bk = nb()
                    mm(ps[:, bk, :], wmix[:, g, :], pooledT[:, pb, si * 512:(si + 1) * 512], True, True, [r_const, r_pooled[pb]], bk)
                    P.emit("act", (lambda e, bk=bk, g=g, si=si: e.activation(out=mixedT[:, g, si * 512:(si + 1) * 512], in_=ps[:, bk, :],
                                                                            func=AF.Copy, scale=pscale[:, g:g + 1])),
                           reads=[r_bank[bk], r_const], writes=[r_mixed[g][si]])


            zproj(0)
            zproj(1)
            kproj(0)
            zproj(2)
            kproj(1)
            zproj(3)
            vproj()
            for g in range(4):
                mixmm(g)

            wq = [None, None]
            rq = [None, None]
            set_ctr = [0]

            def qproj(c):
                if c % 4 == 0:
                    wq[0], rq[0] = load_cols(win_d, (c // 4) * 512, 512)
                qb = c % 2
                sset = set_ctr[0] % 3
                set_ctr[0] += 1
                for si in range(2):
                    bq = sset + 3 * si
                    for kc in range(NKC):
                        mm(ps[:, bq, :], wq[0][:, kc, (c % 4) * 128:(c % 4 + 1) * 128], xnT[:, kc, 128 + si * 512:128 + (si + 1) * 512],
                           kc == 0, kc == NKC - 1, [rq[0]] + [r_xnT[b] for b in range(1 + si * 4, 5 + si * 4)], bq)
                for si in range(2):
                    bq = sset + 3 * si
                    P.emit("act", (lambda e, qb=qb, si=si, bq=bq: e.activation(out=qT[:, qb, si * 512:(si + 1) * 512], in_=ps[:, bq, :], func=AF.Copy)),
                           reads=[r_bank[bq]], writes=[r_qT[qb]])

            att_u = [0]

            def qk(c, up):
                u = att_u[0]
                kvh = c // 4
                qb = c % 2
                sset = set_ctr[0] % 3
                set_ctr[0] += 1
                b = 1 + 2 * up
                for seg in range(4):
                    qblk = b + seg // 2
                    kblk = qblk - 1 + seg % 2
                    for half in range(2):
                        bank = sset + 3 * half
                        p0 = half * 64
                        mm(ps[:, bank, seg * 128:(seg + 1) * 128], KT[p0:p0 + 64, kvh, kblk * 128:(kblk + 1) * 128],
                           qT[p0:p0 + 64, qb, (qblk - 1) * 128:qblk * 128], True, False, [r_KT[kblk], r_qT[qb]], bank)
                    for half in range(2):
                        bank = sset + 3 * half
                        mm(ps[:, bank, seg * 128:(seg + 1) * 128], ident[:], emask[:, c, half, (seg % 2) * 128:(seg % 2 + 1) * 128],
                           False, half == 1, [r_const], bank)
                        if half == 0:
                            mm(ps[:, bank, seg * 128:(seg + 1) * 128], ident[:], emlo[:, c, (seg % 2) * 128:(seg % 2 + 1) * 128],
                               False, True, [r_const], bank)
                for half in range(2):
                    bank = sset + 3 * half
                    pk = (u % 3) * 2 + half
                    P.emit("act", (lambda e, pk=pk, bank=bank: e.activation(out=PT[:, pk, :], in_=ps[:, bank, :], func=AF.Exp, scale=0.125)),
                           reads=[r_bank[bank]], writes=[r_PT[pk]])
                att_u[0] += 1
                return (u, c, up)

            def pv(tok):
                u, c, up = tok
                kvh = c // 4
                par = u % 2
                b = 1 + 2 * up
                bank = 6 + par
                for what in range(2):
                    for qi in range(2):
                        qblk = b + qi
                        col = what * 256 + qi * 128
                        for kk in range(2):
                            kblk = qblk - 1 + kk
                            seg = qi * 2 + kk
                            for half in range(2):
                                pk = (u % 3) * 2 + half
                                if what == 0:
                                    lhsT = V[:, kblk, kvh, :]
                                    rd = [r_V[kblk], r_PT[pk]]
                                else:
                                    lhsT = oneh[:] if (ti == 0 and kblk == 0) else ones[:]
                                    rd = [r_const, r_PT[pk]]
                                mm(ps[half * 64:(half + 1) * 64, bank, col:col + 128], lhsT, PT[:, pk, seg * 128:(seg + 1) * 128],
                                   kk == 0, kk == 1, rd, bank)
                dk = par
                P.emit("dve", (lambda e, dk=dk, bank=bank, c=c: e.tensor_scalar_add(out=dent[:, dk, :], in0=ps[:, bank, 256:512],
                                                                                   scalar1=esink[:, c:c + 1])),
                       reads=[r_bank[bank], r_const], writes=[r_dent[dk]])
                P.emit("dve", (lambda e, dk=dk: e.reciprocal(out=dent[:, dk, :], in_=dent[:, dk, :])),
                       reads=[r_dent[dk]], writes=[r_dent[dk]])
                P.emit("dve", (lambda e, dk=dk, bank=bank, c=c, b=b: e.tensor_tensor(out=R1[:, c, b * 128:(b + 2) * 128], in0=ps[:, bank, 0:256],
                                                                                    in1=dent[:, dk, :], op=ALU.mult)),
                       reads=[r_bank[bank], r_dent[dk]], writes=[r_attnT[c][b], r_attnT[c][b + 1]])

            P.mark(f"t{ti}.attn")
            qproj(0)
            seq = [(c, up) for c in range(8) for up in range(4)]
            toks = []
            for n, (c, up) in enumerate(seq):
                if up == 2 and c < 7:
                    qproj(c + 1)
                toks.append(qk(c, up))
                if n >= 2:
                    pv(toks[n - 2])
            pv(toks[-2])
            pv(toks[-1])

            P.mark(f"t{ti}.m4")
            for mg in range(2):
                wau, rau = load_cols(wau_d, mg * 512, 512)
                wga, rga = load_cols(win_d, 1792 + mg * 512, 512)
                wpu, rpu = load_cols(wpu_d, mg * 512, 512, nk=4)
                wgp, rgp = load_cols(win_d, 2816 + mg * 512, 512)
                for mi in range(4):
                    m = mg * 4 + mi
                    ms = slice(mi * 128, (mi + 1) * 128)
                    for si, (c0, n, bl) in enumerate(subs_main):
                        par = (mi * 2 + si) % 2
                        ba, bga, bp, bgp = (par * 4 + k for k in range(4))
                        xr = [r_xnT[b] for b in bl]
                        for kc in range(NKC):
                            mm(ps[:, ba, :], wau[:, kc, ms], R1[:, kc, c0:c0 + n], kc == 0, kc == NKC - 1, [rau] + [r_attnT[kc][b] for b in bl], ba)
                        for kc in range(NKC):
                            mm(ps[:, bga, :], wga[:, kc, ms], xnT[:, kc, c0:c0 + n], kc == 0, kc == NKC - 1, [rga] + xr, bga)
                        for kc in range(4):
                            mm(ps[:, bp, :], wpu[:, kc, ms], mixedT[:, kc, si * 512:(si + 1) * 512], kc == 0, kc == 3, [rpu, r_mixed[kc][si]], bp)
                        for kc in range(NKC):
                            mm(ps[:, bgp, :], wgp[:, kc, ms], xnT[:, kc, c0:c0 + n], kc == 0, kc == NKC - 1, [rgp] + xr, bgp)
                        t1v, t1r = t32_next()
                        t2v, t2r = t32_next()
                        P.emit("act", (lambda e, t1v=t1v, bga=bga: e.activation(out=t1v, in_=ps[:, bga, :], func=AF.Sigmoid)),
                               reads=[r_bank[bga]], writes=[t1r])
                        P.emit("act", (lambda e, t2v=t2v, bgp=bgp: e.activation(out=t2v, in_=ps[:, bgp, :], func=AF.Sigmoid)),
                               reads=[r_bank[bgp]], writes=[t2r])
                        P.emit("dve", (lambda e, t1v=t1v, ba=ba: e.tensor_tensor(out=t1v, in0=ps[:, ba, :], in1=t1v, op=ALU.mult)),
                               reads=[r_bank[ba], t1r], writes=[t1r])
                        P.emit("dve", (lambda e, t2v=t2v, bp=bp: e.tensor_tensor(out=t2v, in0=ps[:, bp, :], in1=t2v, op=ALU.mult)),
                               reads=[r_bank[bp], t2r], writes=[t2r])
                        P.emit("dve", (lambda e, t1v=t1v, t2v=t2v, m=m, c0=c0, n=n: e.tensor_tensor(out=mergedT[:, m, c0:c0 + n], in0=t1v, in1=t2v,
                                                                                                  op=ALU.add)),
                               reads=[t1r, t2r], writes=[r_merged[m][b] for b in bl])
            P.mark(f"t{ti}.m5")
            wo0, ro0 = load_cols(wo_d, 0, 512)
            wo1, ro1 = load_cols(wo_d, 512, 512)
            for bi, b in enumerate(main):
                if pipe is not None:
                    pipe.pre()
                b0 = 4 + 2 * (bi % 2)
                for hf, (wo, ro) in enumerate(((wo0, ro0), (wo1, ro1))):
                    for kc in range(NKC):
                        mm(ps[:, b0 + hf, :], mergedT[:, kc, b * 128:(b + 1) * 128], wo[:, kc, :], kc == 0, kc == NKC - 1,
                           [ro, r_merged[kc][b]], b0 + hf)
                for hf in range(2):
                    P.emit("dve", (lambda e, b=b, b0=b0, hf=hf: e.tensor_tensor(out=h[:, b, hf * 512:(hf + 1) * 512], in0=ps[:, b0 + hf, :],
                                                                               in1=h[:, b, hf * 512:(hf + 1) * 512], op=ALU.add)),
                           reads=[r_bank[b0 + hf], r_h[b]], writes=[r_h[b]])
                if pipe is not None:
                    pipe.post(b)

        def dump_h(ti):
            for b in range(1, 9):
                row = ti * TT + (b - 1) * 128
                P.emit("sp", (lambda e, b=b, row=row: e.dma_start(out=y_d[row:row + 128, :], in_=h[:, b, :])),
                       reads=[r_h[b]], dsem=h_sems[b])

        def load_x(ti, b):
            row = ti * TT + b * 128
            P.emit("sp", (lambda e, b=b, row=row: e.dma_start(out=h[:, b, :], in_=x_d[row:row + 128, :])),
                   writes=[r_h[b]], dsem=h_sems[b])

        n0 = None
        xq = []
        XLAG = 3
        for ti in range(n_tiles):
            blocks_all = list(range(0, 9)) if ti == 0 else list(range(1, 9))
            main = list(range(1, 9))
            P.mark(f"t{ti}.norm0")
            if n0 is None:
                for b in blocks_all:
                    load_x(ti, b)
                norm(0, blocks_all)
            else:
                while xq:
                    n0.step(xq.pop(0))
                n0.flush()
            P.mark(f"t{ti}.ffn1")
            if debug_stage == 1:
                ffn(0, blocks_all)
                dump_h(ti); n0 = None; continue
            n1 = NormPipe(1)
            ffn(0, blocks_all, pipe=n1)
            P.mark(f"t{ti}.norm1")
            n1.flush()
            n2 = NormPipe(2)
            mixer(ti, pipe=(None if debug_stage == 2 else n2))
            if debug_stage == 2:
                dump_h(ti); n0 = None; continue
            P.mark(f"t{ti}.norm2")
            n2.flush()
            P.mark(f"t{ti}.ffn2")
            if ti + 1 < n_tiles:
                n0 = NormPipe(0)

                def after_store(b, ti=ti, n0=n0):
                    load_x(ti + 1, b)
                    xq.append(b)
                    if len(xq) > XLAG:
                        n0.step(xq.pop(0))
            else:
                n0 = None
                after_store = None
            n3 = NormPipe(3, out_rows=(lambda b, ti=ti: ti * TT + (b - 1) * 128), after_store=after_store)
            ffn(1, main, pipe=n3)
            n3.flush()

        for eng in ("pe", "act", "dve"):
            n = 0
            for ins in P.q[eng]:
                if ins.awaited:
                    ins.sem = eng_sems[eng][n // SEM_CH]
                    ins.val = n % SEM_CH + 1
                    n += 1
            assert n <= SEM_CH * n_eng_sems, (eng, n)
        for eng in ("pool", "sp"):
            n_const = sum(1 for ins in P.q[eng] if ins.dsem is const_sems[eng])
            for ins in P.q[eng]:
                k = id(ins.dsem)
                dsem_count[k] = dsem_count.get(k, 0) + 16
                ins.sem = ins.dsem
                ins.val = 16 * n_const if ins.dsem is const_sems[eng] else dsem_count[k]
        last_out = {}
        for ins in P.q["sp"]:
            last_out[id(ins.sem)] = ins

        block = E(nc.Block())

        def replay(engname):
            def run(e):
                waited = {}
                for ins in P.q[engname]:
                    for t in ins.waits:
                        k = id(t.sem)
                        if waited.get(k, 0) < t.val:
                            e.wait_ge(t.sem, t.val)
                            waited[k] = t.val
                    bi = ins.fn(e)
                    if ins.is_dma:
                        bi.then_inc(ins.sem, 16)
                    elif ins.awaited:
                        bi.then_inc(ins.sem, 1)
                if engname == "sp":
                    for ins in last_out.values():
                        e.wait_ge(ins.sem, ins.val)
            return run

        block.gpsimd(replay("pool"))
        block.sync(replay("sp"))
        block.scalar(replay("act"))
        block.vector(replay("dve"))
        block.tensor(replay("pe"))
    nc._marks = P.marks
    return nc


def _alibi_slopes():
    hh = np.arange(1, 17, dtype=np.float32)
    return (2.0 ** (-8.0 * hh / 16)).astype(np.float32)


def _bf16_round(a):
    u = np.ascontiguousarray(a, dtype=np.float32).view(np.uint32).astype(np.uint64)
    u = (u + 0x7FFF + ((u >> 16) & 1)) & 0xFFFF0000
    return u.astype(np.uint32).view(np.float32)


def _host_constants():
    sl = _alibi_slopes()
    k = np.arange(128)[:, None].astype(np.float32)
    q = np.arange(128)[None, :].astype(np.float32)
    em = np.zeros((128, 8, 2, 256), np.float32)
    for c in range(8):
        for half in range(2):
            s = sl[2 * c + half]
            dprev = 128.0 + q - k
            dcur = q - k
            em[:, c, half, 0:128] = np.where(dprev < 128, -8.0 * s * dprev, -240000.0)
            em[:, c, half, 128:256] = np.where(dcur >= 0, -8.0 * s * dcur, -240000.0)
    hi = _bf16_round(em)
    lo = np.where(em <= -200000.0, 0.0, _bf16_round(em - hi)).astype(np.float32)
    assert np.all(lo[:, :, 1, :] == 0.0)
    return hi.reshape(128, -1), np.ascontiguousarray(lo[:, :, 0, :]).reshape(128, -1), np.eye(128, dtype=np.float32)


_NC_CACHE = {}


def kernel(x, ffn1_norm, ffn1_w_up, ffn1_w_down, mix_norm, w_in, sinks, w_attn_up,
           pool_w_mix, pool_scale, w_pool_up, w_out, ffn2_norm, ffn2_w_up, ffn2_w_down,
           final_norm, _debug_stage=None, _n_tiles=NT):
    f = lambda a: np.ascontiguousarray(np.asarray(a, dtype=np.float32))
    x = f(x)
    Bn, S, _ = x.shape
    em, emlo, ident = _host_constants()
    gcols = np.stack([f(ffn1_norm)[0].reshape(8, 128).T, f(mix_norm)[0].reshape(8, 128).T,
                      f(ffn2_norm)[0].reshape(8, 128).T], axis=1).reshape(128, 24)
    gfin = np.ascontiguousarray(np.broadcast_to(f(final_norm)[None, :], (128, D)))
    sk = f(sinks)[0]
    sinkc = np.ascontiguousarray(np.repeat(sk.reshape(8, 2), 64, axis=1).T)
    pscale = np.ascontiguousarray(f(pool_scale)[0].reshape(4, 128).T)
    shared = {
        "w_up1": f(ffn1_w_up)[0], "w_dn1": f(ffn1_w_down)[0], "w_up2": f(ffn2_w_up)[0], "w_dn2": f(ffn2_w_down)[0],
        "w_in": f(w_in)[0], "w_au": f(w_attn_up)[0], "w_pu": f(w_pool_up)[0], "w_o": f(w_out)[0],
        "w_mix": f(pool_w_mix)[0], "emask": em, "emlo": emlo, "ident": ident, "gcols": np.ascontiguousarray(gcols), "gfin": gfin,
        "sinkc": sinkc, "pscale": pscale,
    }
    in_maps = []
    for core in range(8):
        b, half = core // 2, core % 2
        xc = np.zeros((HALO + TOK_CORE, D), np.float32)
        if half == 0:
            xc[HALO:] = x[b, 0:TOK_CORE]
        else:
            xc[:] = x[b, TOK_CORE - HALO:S]
        pinv = np.zeros((128, 4, 16), np.float32)
        for g, w in enumerate(POOL_W):
            t = np.arange(16)
            pinv[:, g, :] = (1.0 / np.minimum(t + 1, w)) if half == 0 else (1.0 / w)
        oneh = np.full((128, 64), 1.0 if half == 1 else 0.0, np.float32)
        m = dict(shared)
        m["x"] = xc
        m["pinv"] = pinv.reshape(128, 64)
        m["oneh"] = oneh
        in_maps.append(m)
    key = (_debug_stage, _n_tiles)
    if key not in _NC_CACHE:
        _NC_CACHE[key] = build_program(_debug_stage, _n_tiles)
    nc = _NC_CACHE[key]
    res = run_bass_kernel_spmd(nc, in_maps, core_ids=list(range(8)))
    out = np.empty((Bn, S, D), np.float32)
    for core in range(8):
        b, half = core // 2, core % 2
        out[b, half * TOK_CORE:(half + 1) * TOK_CORE] = res.results[core]["y"]
    return out
```
